# Optimizing a Trainium2 kernel written in Bass

```python
import math
import jax
import jax.numpy as jnp
from jax import lax
import numpy as np

D_MODEL = 2048
BATCH = 4
SEQ = 2048
DEPTH = 1
DEC_BATCH = 128
DEC_SEQ = 1
PAST_LEN = 8192
PAGE_SIZE = 128

N_HEADS = 16
N_KV_HEADS = 4
HEAD_DIM = 64
Q_PER_KV = N_HEADS // N_KV_HEADS
D_ATTN = N_HEADS * HEAD_DIM
D_KV = N_KV_HEADS * HEAD_DIM
WINDOW = 128
SSD_HEADS = 32
SSD_HEAD_DIM = 64
D_SSD = SSD_HEADS * SSD_HEAD_DIM
SSD_GROUPS = 4
HEADS_PER_GROUP = SSD_HEADS // SSD_GROUPS
D_STATE = 128
SSD_CONV = 4
SSD_CHUNK = 128
CONV_DIM = D_SSD + 2 * SSD_GROUPS * D_STATE
D_MIX = D_ATTN + D_SSD
IN_SPLITS = (D_ATTN, D_ATTN + D_KV, D_ATTN + 2 * D_KV, D_ATTN + 2 * D_KV + D_SSD,
             D_ATTN + 2 * D_KV + D_SSD + CONV_DIM)
D_IN = D_ATTN + 2 * D_KV + D_SSD + CONV_DIM + SSD_HEADS
N_MEM = 256
MEM_HEADS = 4
MEM_HEAD_DIM = 128
D_XATTN = MEM_HEADS * MEM_HEAD_DIM
D_FF = 11 * D_MODEL // 4
FFN_CONV = 3
EPS = 1e-6

kernel_name = 'hymba_swa_ssd_memxattn_convffn_step'


def rms_norm(x, g):
    xf = x.astype(jnp.float32)
    y = xf * lax.rsqrt(jnp.mean(xf * xf, axis=-1, keepdims=True) + EPS)
    return (y * g.astype(jnp.float32)).astype(x.dtype)


def causal_dwconv(u, past, w, b):
    k, t = w.shape[0], u.shape[1]
    full = jnp.concatenate([past.astype(u.dtype), u], axis=1)
    out = sum((full[:, i:i + t] * w[i] for i in range(k)), b)
    return out, full[:, t:]


def sink_softmax(logits, mask, sink):
    logits = jnp.where(mask, logits, -jnp.inf)
    sink = jnp.broadcast_to(sink.astype(jnp.float32), logits.shape[:-1] + (1,))
    return jax.nn.softmax(jnp.concatenate([logits, sink], axis=-1), axis=-1)[..., :-1]


def swa_banded(q, k, v, sinks):
    b, t = q.shape[:2]
    nb = t // WINDOW
    qb = q.reshape(b, nb, WINDOW, N_KV_HEADS, Q_PER_KV, HEAD_DIM)

    def band(a):
        a = a.reshape(b, nb, WINDOW, N_KV_HEADS, HEAD_DIM)
        prev = jnp.concatenate([jnp.zeros_like(a[:, :1]), a[:, :-1]], axis=1)
        return jnp.concatenate([prev, a], axis=2)

    kk, vv = band(k), band(v)
    logits = jnp.einsum('bnqkgd,bnskd->bnkgqs', qb, kk,
                        preferred_element_type=jnp.float32) * HEAD_DIM ** -0.5
    qi = jnp.arange(WINDOW)[:, None]
    sj = jnp.arange(2 * WINDOW)[None, :]
    diff = qi + WINDOW - sj
    blk = jnp.arange(nb)[:, None, None]
    mask = ((diff >= 0) & (diff <= WINDOW))[None] & ((blk > 0) | (sj >= WINDOW)[None])
    p = sink_softmax(logits, mask[None, :, None, None],
                     sinks.reshape(1, 1, N_KV_HEADS, Q_PER_KV, 1, 1))
    o = jnp.einsum('bnkgqs,bnskd->bnqkgd', p.astype(v.dtype), vv)
    return o.reshape(b, t, D_ATTN)


def swa_step(q, k, v, k_buf, v_buf, sinks):
    b, t = q.shape[:2]
    w = k_buf.shape[1]
    kk = jnp.concatenate([k_buf.astype(k.dtype), k], axis=1)
    vv = jnp.concatenate([v_buf.astype(v.dtype), v], axis=1)
    qg = q.reshape(b, t, N_KV_HEADS, Q_PER_KV, HEAD_DIM)
    logits = jnp.einsum('bqkgd,bskd->bkgqs', qg, kk,
                        preferred_element_type=jnp.float32) * HEAD_DIM ** -0.5
    diff = jnp.arange(t)[:, None] + w - jnp.arange(w + t)[None, :]
    mask = (diff >= 0) & (diff <= WINDOW)
    p = sink_softmax(logits, mask, sinks.reshape(1, N_KV_HEADS, Q_PER_KV, 1, 1))
    o = jnp.einsum('bkgqs,bskd->bqkgd', p.astype(v.dtype), vv).reshape(b, t, D_ATTN)
    return o, kk[:, t:], vv[:, t:]


def ssd_scan(xs, dt, a, bm, cm, h0, chunk):
    b, t = xs.shape[:2]
    nc = t // chunk
    xs = xs.reshape(b, nc, chunk, *xs.shape[2:])
    dt = dt.reshape(b, nc, chunk, *dt.shape[2:])
    bm = bm.reshape(b, nc, chunk, *bm.shape[2:])
    cm = cm.reshape(b, nc, chunk, *cm.shape[2:])
    la = jnp.cumsum(dt * a, axis=2)
    xdt = xs * dt[..., None]
    causal = jnp.tril(jnp.ones((chunk, chunk), dtype=bool))[:, :, None, None]
    seg = la[:, :, :, None] - la[:, :, None, :]
    decay = jnp.exp(jnp.where(causal, seg, -jnp.inf))
    cb = jnp.einsum('bclgn,bcsgn->bclsg', cm, bm)
    y = jnp.einsum('bclsgh,bcsghp->bclghp', cb[..., None] * decay, xdt)
    decay_end = jnp.exp(la[:, :, -1:] - la)
    states = jnp.einsum('bcsgn,bcsghp->bcghpn', bm, xdt * decay_end[..., None])
    chunk_decay = jnp.exp(la[:, :, -1])

    def carry_step(h, inp):
        s_c, d_c = inp
        return h * d_c[..., None, None] + s_c, h

    h_last, h_prev = lax.scan(carry_step, h0,
                              (jnp.moveaxis(states, 1, 0), jnp.moveaxis(chunk_decay, 1, 0)))
    h_prev = jnp.moveaxis(h_prev, 0, 1)
    y = y + jnp.einsum('bclgn,bcghpn->bclghp', cm, h_prev) * jnp.exp(la)[..., None]
    return y.reshape(b, t, *y.shape[3:]), h_last


def ssd_mixer(z, xbc, dt_raw, conv_past, h0, conv_w, conv_b, dt_bias, a_log, d_skip, norm_g):
    b, t = z.shape[:2]
    xbc, conv_new = causal_dwconv(xbc, conv_past, conv_w, conv_b)
    xbc = jax.nn.silu(xbc.astype(jnp.float32))
    xs, bm, cm = jnp.split(xbc, (D_SSD, D_SSD + SSD_GROUPS * D_STATE), axis=-1)
    dt = jax.nn.softplus(dt_raw.astype(jnp.float32) + dt_bias.astype(jnp.float32))
    a = -jnp.exp(a_log.astype(jnp.float32))
    chunk = SSD_CHUNK if t % SSD_CHUNK == 0 else t
    xs = xs.reshape(b, t, SSD_GROUPS, HEADS_PER_GROUP, SSD_HEAD_DIM)
    y, h_last = ssd_scan(
        xs,
        dt.reshape(b, t, SSD_GROUPS, HEADS_PER_GROUP),
        a.reshape(SSD_GROUPS, HEADS_PER_GROUP),
        bm.reshape(b, t, SSD_GROUPS, D_STATE),
        cm.reshape(b, t, SSD_GROUPS, D_STATE),
        h0.astype(jnp.float32).reshape(b, SSD_GROUPS, HEADS_PER_GROUP, SSD_HEAD_DIM, D_STATE),
        chunk)
    y = y + xs * d_skip.astype(jnp.float32).reshape(SSD_GROUPS, HEADS_PER_GROUP, 1)
    y = y.reshape(b, t, D_SSD) * jax.nn.silu(z.astype(jnp.float32))
    y = rms_norm(y, norm_g).astype(z.dtype)
    h_last = h_last.reshape(b, SSD_HEADS, SSD_HEAD_DIM, D_STATE).astype(h0.dtype)
    return y, conv_new, h_last


def memory_kv(mem, g_mem, w_ck, w_cv, ck_norm_g):
    b, m = mem.shape[:2]
    h = rms_norm(mem, g_mem)
    k = rms_norm((h @ w_ck).reshape(b, m, MEM_HEADS, MEM_HEAD_DIM), ck_norm_g)
    v = (h @ w_cv).reshape(b, m, MEM_HEADS, MEM_HEAD_DIM)
    return k, v


def cross_attend(x, mem_k, mem_v, g_cross, w_cq, cq_norm_g, w_co):
    b, t = x.shape[:2]
    q = rms_norm((rms_norm(x, g_cross) @ w_cq).reshape(b, t, MEM_HEADS, MEM_HEAD_DIM), cq_norm_g)
    logits = jnp.einsum('bqhd,bshd->bhqs', q, mem_k.astype(q.dtype),
                        preferred_element_type=jnp.float32) * MEM_HEAD_DIM ** -0.5
    p = jax.nn.softmax(logits, axis=-1)
    o = jnp.einsum('bhqs,bshd->bqhd', p.astype(x.dtype), mem_v.astype(x.dtype))
    return o.reshape(b, t, D_XATTN) @ w_co


def conv_ffn(x, past, g_ffn, w_up, conv_w, conv_b, w_down):
    u, new_past = causal_dwconv(rms_norm(x, g_ffn) @ w_up, past, conv_w, conv_b)
    gate, val = jnp.split(u, 2, axis=-1)
    return (jax.nn.silu(gate) * val) @ w_down, new_past


def _layer(x, mem_k, mem_v, k_buf, v_buf, ssd_conv_past, ssd_h0, ffn_past, w):
    b, t = x.shape[:2]
    proj = rms_norm(x, w['g_mix']) @ w['w_in']
    q, k, v, z, xbc, dt_raw = jnp.split(proj, IN_SPLITS, axis=-1)
    q = rms_norm(q.reshape(b, t, N_HEADS, HEAD_DIM), w['q_norm_g'])
    k = rms_norm(k.reshape(b, t, N_KV_HEADS, HEAD_DIM), w['k_norm_g'])
    v = v.reshape(b, t, N_KV_HEADS, HEAD_DIM)
    if k_buf is None:
        attn = swa_banded(q, k, v, w['sinks'])
        new_k, new_v = k[:, -WINDOW:], v[:, -WINDOW:]
    else:
        attn, new_k, new_v = swa_step(q, k, v, k_buf, v_buf, w['sinks'])
    ssd, new_conv, new_h = ssd_mixer(z, xbc, dt_raw, ssd_conv_past, ssd_h0, w['ssd_conv_w'],
                                     w['ssd_conv_b'], w['dt_bias'], w['a_log'], w['d_skip'],
                                     w['ssd_norm_g'])
    x = x + jnp.concatenate([attn, ssd], axis=-1) @ w['w_out']
    x = x + cross_attend(x, mem_k, mem_v, w['g_cross'], w['w_cq'], w['cq_norm_g'], w['w_co'])
    f, new_ffn = conv_ffn(x, ffn_past, w['g_ffn'], w['w_up'], w['ffn_conv_w'], w['ffn_conv_b'],
                          w['w_down'])
    return x + f, (new_k, new_v, new_conv, new_h, new_ffn)


def setup_inputs(seed: int = 0) -> dict:
    key = jax.random.key(seed)
    keys = iter(jax.random.split(key, 48))
    L = DEPTH
    swa_buf = min(WINDOW, PAST_LEN)

    def nrm(shape, scale=1.0):
        return jax.random.normal(next(keys), shape, jnp.float32) * scale

    def gain(n):
        return 1.0 + nrm((L, n), 0.05)

    dt0 = jnp.exp(jax.random.uniform(next(keys), (L, SSD_HEADS), jnp.float32,
                                     math.log(1e-3), math.log(1e-1)))
    dt_bias = dt0 + jnp.log(-jnp.expm1(-dt0))
    a_log = jnp.log(jax.random.uniform(next(keys), (L, SSD_HEADS), jnp.float32, 1.0, 16.0))
    return {
        'x_prompt': nrm((BATCH, SEQ, D_MODEL)),
        'x_sample': nrm((DEC_BATCH, DEC_SEQ, D_MODEL)),
        'cache_swa_k': nrm((L, DEC_BATCH, swa_buf, N_KV_HEADS, HEAD_DIM)),
        'cache_swa_v': nrm((L, DEC_BATCH, swa_buf, N_KV_HEADS, HEAD_DIM)),
        'state_ssd_conv': nrm((L, DEC_BATCH, SSD_CONV - 1, CONV_DIM)),
        'state_ssd': nrm((L, DEC_BATCH, SSD_HEADS, SSD_HEAD_DIM, D_STATE), 0.5),
        'cache_mem_k': nrm((L, DEC_BATCH, N_MEM, MEM_HEADS, MEM_HEAD_DIM)),
        'cache_mem_v': nrm((L, DEC_BATCH, N_MEM, MEM_HEADS, MEM_HEAD_DIM)),
        'state_ffn_conv': nrm((L, DEC_BATCH, FFN_CONV - 1, 2 * D_FF)),
        'mem_prompt': nrm((BATCH, N_MEM, D_MODEL)),
        'g_mix': gain(D_MODEL),
        'w_in': nrm((L, D_MODEL, D_IN), D_MODEL ** -0.5),
        'q_norm_g': gain(HEAD_DIM),
        'k_norm_g': gain(HEAD_DIM),
        'sinks': nrm((L, N_HEADS)),
        'ssd_conv_w': nrm((L, SSD_CONV, CONV_DIM), SSD_CONV ** -0.5),
        'ssd_conv_b': nrm((L, CONV_DIM), 0.02),
        'dt_bias': dt_bias,
        'a_log': a_log,
        'd_skip': 1.0 + nrm((L, SSD_HEADS), 0.1),
        'ssd_norm_g': gain(D_SSD),
        'w_out': nrm((L, D_MIX, D_MODEL), D_MIX ** -0.5),
        'g_cross': gain(D_MODEL),
        'g_mem': gain(D_MODEL),
        'w_cq': nrm((L, D_MODEL, D_XATTN), D_MODEL ** -0.5),
        'w_ck': nrm((L, D_MODEL, D_XATTN), D_MODEL ** -0.5),
        'w_cv': nrm((L, D_MODEL, D_XATTN), D_MODEL ** -0.5),
        'cq_norm_g': gain(MEM_HEAD_DIM),
        'ck_norm_g': gain(MEM_HEAD_DIM),
        'w_co': nrm((L, D_XATTN, D_MODEL), D_XATTN ** -0.5),
        'g_ffn': gain(D_MODEL),
        'w_up': nrm((L, D_MODEL, 2 * D_FF), D_MODEL ** -0.5),
        'ffn_conv_w': nrm((L, FFN_CONV, 2 * D_FF), FFN_CONV ** -0.5),
        'ffn_conv_b': nrm((L, 2 * D_FF), 0.02),
        'w_down': nrm((L, D_FF, D_MODEL), D_FF ** -0.5),
    }


def reference(x_prompt, x_sample, cache_swa_k, cache_swa_v, state_ssd_conv, state_ssd,
              cache_mem_k, cache_mem_v, state_ffn_conv, mem_prompt,
              g_mix, w_in, q_norm_g, k_norm_g, sinks, ssd_conv_w, ssd_conv_b, dt_bias, a_log,
              d_skip, ssd_norm_g, w_out, g_cross, g_mem, w_cq, w_ck, w_cv, cq_norm_g, ck_norm_g,
              w_co, g_ffn, w_up, ffn_conv_w, ffn_conv_b, w_down):
    y_prompt, y_sample = x_prompt, x_sample
    bp = x_prompt.shape[0]
    new_p, new_s = [], []
    for l in range(DEPTH):
        w = dict(g_mix=g_mix[l], w_in=w_in[l], q_norm_g=q_norm_g[l], k_norm_g=k_norm_g[l],
                 sinks=sinks[l], ssd_conv_w=ssd_conv_w[l], ssd_conv_b=ssd_conv_b[l],
                 dt_bias=dt_bias[l], a_log=a_log[l], d_skip=d_skip[l], ssd_norm_g=ssd_norm_g[l],
                 w_out=w_out[l], g_cross=g_cross[l], w_cq=w_cq[l], cq_norm_g=cq_norm_g[l],
                 w_co=w_co[l], g_ffn=g_ffn[l], w_up=w_up[l], ffn_conv_w=ffn_conv_w[l],
                 ffn_conv_b=ffn_conv_b[l], w_down=w_down[l])
        mk_p, mv_p = memory_kv(mem_prompt, g_mem[l], w_ck[l], w_cv[l], ck_norm_g[l])
        dt_x = x_prompt.dtype
        y_prompt, st_p = _layer(
            y_prompt, mk_p, mv_p, None, None,
            jnp.zeros((bp, SSD_CONV - 1, CONV_DIM), dt_x),
            jnp.zeros((bp, SSD_HEADS, SSD_HEAD_DIM, D_STATE), dt_x),
            jnp.zeros((bp, FFN_CONV - 1, 2 * D_FF), dt_x), w)
        y_sample, st_s = _layer(
            y_sample, cache_mem_k[l], cache_mem_v[l], cache_swa_k[l], cache_swa_v[l],
            state_ssd_conv[l], state_ssd[l], state_ffn_conv[l], w)
        new_p.append(st_p + (mk_p, mv_p))
        new_s.append(st_s)
    (swa_k_prompt, swa_v_prompt, ssd_conv_prompt, ssd_state_prompt, ffn_conv_prompt,
     mem_k_prompt, mem_v_prompt) = [jnp.stack(a) for a in zip(*new_p)]
    (swa_k_sample, swa_v_sample, ssd_conv_sample, ssd_state_sample,
     ffn_conv_sample) = [jnp.stack(a) for a in zip(*new_s)]
    return (y_prompt, y_sample, swa_k_prompt, swa_v_prompt, swa_k_sample, swa_v_sample,
            ssd_conv_prompt, ssd_conv_sample, ssd_state_prompt, ssd_state_sample,
            mem_k_prompt, mem_v_prompt, ffn_conv_prompt, ffn_conv_sample)
```

```python
import numpy as np
import concourse.bass as bass
import concourse.mybir as mybir
from concourse.bass_utils import run_bass_kernel_spmd
from contextlib import ExitStack

F32 = mybir.dt.float32
BF16 = mybir.dt.bfloat16
AF = mybir.ActivationFunctionType
ALU = mybir.AluOpType
AX = mybir.AxisListType

EPS = 1e-6
N_HEADS, N_KV, HD = 16, 4, 64
SSD_H, SSD_P, SSD_G, DST = 32, 64, 4, 128
D_ATTN, D_KV, D_SSD = 1024, 256, 2048
CONV_DIM = D_SSD + 2 * SSD_G * DST
D_IN = D_ATTN + 2 * D_KV + D_SSD + CONV_DIM + SSD_H
D_MIX = D_ATTN + D_SSD
MEM_H, MEM_D = 4, 128
D_X = 512
C_Q, C_K, C_V, C_Z, C_XBC, C_DT = 0, 1024, 1280, 1536, 3584, 6656


class _Stop(Exception):
    pass


class Cfg:
    def __init__(self, D=2048, SEQ=2048, NS=16, DFF=5632, NMEM=256, NTG=2, stop=None):
        self.stop = stop
        self.D, self.SEQ, self.NS, self.DFF, self.NMEM, self.NTG = D, SEQ, NS, DFF, NMEM, NTG
        self.KC = D // 128
        self.FC = DFF // 128
        self.FC2 = 2 * self.FC
        self.NT = SEQ // 128
        self.MB = NMEM // 128
        o = 0
        self.po = {}
        for name, w in [("g_mix", self.KC), ("g_cross", self.KC), ("g_ffn", self.KC), ("g_mem", self.KC),
                        ("ssd_norm_g", 16), ("ssd_conv_w", 96), ("ssd_conv_b", 24),
                        ("ffn_conv_w", 3 * self.FC2), ("ffn_conv_b", self.FC2),
                        ("q_norm_g", 1), ("k_norm_g", 1), ("cq_norm_g", 1), ("ck_norm_g", 1),
                        ("sinks", 16), ("dt_bias", 1), ("a_log", 1), ("d_skip", 16)]:
            self.po[name] = (o, w)
            o += w
        self.PW = o


ENGS = ("pe", "act", "dve", "pool", "sp")


class Prog:
    def __init__(self, nc, es):
        self.nc = nc
        self.ops = {e: [] for e in ENGS}
        self.lastw = {}
        self.rd_c = {}
        self.rd_d = {}
        self.esem = {e: es.enter_context(nc.semaphore("sem_" + e)) for e in ENGS if e != "sp"}
        self.ndsem = 40
        self.dsems = [es.enter_context(nc.semaphore("dsem%d" % i)) for i in range(self.ndsem)]
        self.dsem_val = [0] * self.ndsem
        self.dsem_last = [None] * self.ndsem
        self.dnext = {"sp": 0, "pool": 0, "act": 0}
        self.drange = {"sp": (0, 26), "pool": (26, 40), "act": (0, 26)}
        self.bank_i = 0

    ALIAS = {"zs": "R", "xc": "R", "qn": "R", "gT": "R"}

    def op(self, eng, fn, r=(), w=(), dma=False):
        r = [self.ALIAS.get(k, k) for k in r]
        w = [self.ALIAS.get(k, k) for k in w]
        if eng != "pe":
            w = w + [k for k in r if k.startswith("ps") and k[2:].isdigit() and k not in w]
        deps = set()
        for k in r:
            lw = self.lastw.get(k)
            if lw is not None:
                deps.add(lw)
        for k in w:
            lw = self.lastw.get(k)
            if lw is not None:
                deps.add(lw)
            for e2, i2 in self.rd_c.get(k, {}).items():
                deps.add((e2, i2))
            for idn in self.rd_d.get(k, ()):
                deps.add(idn)
        o = dict(eng=eng, fn=fn, deps=deps, dma=dma, idx=len(self.ops[eng]), signal=False)
        ident = (eng, o["idx"])
        if dma:
            lo, hi = self.drange[eng]
            s = lo + self.dnext[eng]
            self.dnext[eng] = (self.dnext[eng] + 1) % (hi - lo)
            if self.dsem_last[s] is not None:
                deps.add(self.dsem_last[s])
            self.dsem_val[s] += 16
            o["dsem"] = (s, self.dsem_val[s])
            self.dsem_last[s] = ident
        deps.discard(ident)
        self.ops[eng].append(o)
        for k in w:
            self.lastw[k] = ident
            self.rd_c[k] = {}
            self.rd_d[k] = []
        for k in r:
            if dma:
                self.rd_d.setdefault(k, []).append(ident)
            else:
                self.rd_c.setdefault(k, {})[eng] = o["idx"]
        return ident

    def emit(self, block_fns):
        for e in ENGS:
            for o in self.ops[e]:
                for (se, si) in o["deps"]:
                    so = self.ops[se][si]
                    if not so["dma"]:
                        if se == "pe" and e == "pe" and not o["dma"]:
                            continue
                        so["signal"] = True
        for e in ENGS:
            c = 0
            for o in self.ops[e]:
                if o["signal"]:
                    c += 1
                    o["sigval"] = c
        ops, esem, dsems = self.ops, self.esem, self.dsems

        def run(ename, eng):
            waited = {}
            for o in ops[ename]:
                for (se, si) in sorted(o["deps"]):
                    so = ops[se][si]
                    if so["dma"]:
                        s, v = so["dsem"]
                        key = ("d", s)
                        sem = dsems[s]
                    else:
                        if se == "pe" and ename == "pe" and not o["dma"]:
                            continue
                        key = ("e", se)
                        sem = esem[se]
                        v = so["sigval"]
                    if waited.get(key, 0) >= v:
                        continue
                    waited[key] = v
                    eng.wait_ge(sem, v)
                ins = o["fn"](eng)
                if o["dma"]:
                    ins.then_inc(dsems[o["dsem"][0]], 16)
                elif o["signal"]:
                    ins.then_inc(esem[ename], 1)
            final = {}
            for o in ops[ename]:
                if o["dma"]:
                    s, v = o["dsem"]
                    final[s] = max(final.get(s, 0), v)
            for s, v in sorted(final.items()):
                eng.wait_ge(dsems[s], v)

        for ename in ENGS:
            block_fns[ename](lambda eng, ename=ename: run(ename, eng))


def build(cfg):
    nc = bass.Bass("TRN2", target_bir_lowering=False)
    D, SEQ, NS, DFF, NMEM, NTG = cfg.D, cfg.SEQ, cfg.NS, cfg.DFF, cfg.NMEM, cfg.NTG
    KC, FC, FC2, NT, MB, PW = cfg.KC, cfg.FC, cfg.FC2, cfg.NT, cfg.MB, cfg.PW

    def din(name, shape):
        return nc.dram_tensor(name, list(shape), F32, kind="ExternalInput").ap()

    def dout(name, shape):
        return nc.dram_tensor(name, list(shape), F32, kind="ExternalOutput").ap()

    xp = din("x_prompt", [SEQ, D]); xs_in = din("x_sample", [NS, D])
    ck_in = din("cache_swa_k", [NS, 128, D_KV]); cv_in = din("cache_swa_v", [NS, 128, D_KV])
    sconv_in = din("state_ssd_conv", [NS, 3, CONV_DIM]); sst_in = din("state_ssd", [NS, D_SSD, DST])
    cmk_in = din("cache_mem_k", [NS, NMEM, D_X]); cmv_in = din("cache_mem_v", [NS, NMEM, D_X])
    sffn_in = din("state_ffn_conv", [NS, 2, 2 * DFF]); memp = din("mem_prompt", [NMEM, D])
    w_in = din("w_in", [D, D_IN]); w_out = din("w_out", [D_MIX, D]); w_cq = din("w_cq", [D, D_X])
    w_ck = din("w_ck", [D, D_X]); w_cv = din("w_cv", [D, D_X]); w_co = din("w_co", [D_X, D])
    w_up = din("w_up", [D, 2 * DFF]); w_down = din("w_down", [DFF, D])
    consts_in = din("consts", [128, 640]); params_in = din("params", [128, PW])

    yp = dout("y_prompt", [SEQ, D]); ys = dout("y_sample", [NS, D])
    o_kp = dout("swa_k_prompt", [128, D_KV]); o_vp = dout("swa_v_prompt", [128, D_KV])
    o_ks = dout("swa_k_sample", [NS, 128, D_KV]); o_vs = dout("swa_v_sample", [NS, 128, D_KV])
    o_cp = dout("ssd_conv_prompt", [3, CONV_DIM]); o_cs = dout("ssd_conv_sample", [NS, 3, CONV_DIM])
    o_sp = dout("ssd_state_prompt", [D_SSD, DST]); o_ss = dout("ssd_state_sample", [NS, D_SSD, DST])
    o_mk = dout("mem_k_prompt", [NMEM, D_X]); o_mv = dout("mem_v_prompt", [NMEM, D_X])
    o_fp = dout("ffn_conv_prompt", [2, 2 * DFF]); o_fs = dout("ffn_conv_sample", [NS, 2, 2 * DFF])

    es = ExitStack()
    with es:
        def sb(name, shape, dt=F32):
            return es.enter_context(nc.sbuf_tensor(name, list(shape), dt))

        P = Prog(nc, es)
        NG = NTG * 128
        ps = [es.enter_context(nc.psum_tensor("ps%d" % i, [128, 512], F32)) for i in range(8)]

        def bank():
            i = P.bank_i
            P.bank_i = (i + 1) % 8
            return i

        cst = sb("cst", [128, 640]); prm = sb("prm", [128, PW])
        ident_bf = sb("ident_bf", [128, 128], BF16)
        mcur4 = sb("mcur4", [128, 4, 128], BF16); mprev4 = sb("mprev4", [128, 4, 128], BF16)
        bones_bf = sb("bones_bf", [128, 128], BF16); ones_bf = sb("ones_bf", [128, 128], BF16)
        ones_f = sb("ones_f", [128, 128]); epsb = sb("epsb", [128, 1]); oneb = sb("oneb", [128, 1])
        esink = sb("esink", [128, 16]); a_neg = sb("a_neg", [128, 1])
        ident_f = cst[:, 0:128]; tri_f = cst[:, 128:256]; U_f = cst[:, 384:512]

        def pcol(name, j=0, n=1):
            o, w = cfg.po[name]
            return prm[:, o + j:o + j + n]

        WB = 2048
        wbufs = [sb("wbuf%d" % i, [128, WB], BF16) for i in range(2)]
        wstg = [sb("wstg%d" % i, [128, WB], F32) for i in range(2)]
        wstate = {"i": 0}

        A = P.op

        def ck(n):
            if cfg.stop == n:
                raise _Stop()

        try:
            A("sp", lambda e: e.dma_start(out=cst[:], in_=consts_in[:, :]), w=["cst"], dma=True)
            A("sp", lambda e: e.dma_start(out=prm[:], in_=params_in[:, :]), w=["prm"], dma=True)
            A("dve", lambda e: e.tensor_copy(out=ident_bf[:], in_=ident_f), r=["cst"], w=["ident_bf"])
            for g in range(4):
                A("dve", lambda e, g=g: e.tensor_copy(out=mcur4[:, g, :], in_=cst[:, 128:256]), r=["cst"], w=["mcur4"])
                A("dve", lambda e, g=g: e.tensor_copy(out=mprev4[:, g, :], in_=cst[:, 256:384]), r=["cst"], w=["mprev4"])
            A("dve", lambda e: e.tensor_copy(out=bones_bf[:], in_=cst[:, 512:640]), r=["cst"], w=["bones_bf"])
            A("dve", lambda e: e.memset(ones_bf[:], 1.0), w=["ones_bf"])
            A("dve", lambda e: e.memset(ones_f[:], 1.0), w=["ones_f"])
            A("dve", lambda e: e.memset(epsb[:], EPS), w=["epsb"])
            A("dve", lambda e: e.memset(oneb[:], 1.0), w=["oneb"])
            A("act", lambda e: e.activation(out=esink[:], in_=pcol("sinks", 0, 16), func=AF.Exp), r=["prm"], w=["esink"])
            A("act", lambda e: e.activation(out=a_neg[:], in_=pcol("a_log"), func=AF.Exp), r=["prm"], w=["a_neg"])
            A("dve", lambda e: e.tensor_scalar(out=a_neg[:], in0=a_neg[:], scalar1=-1.0, scalar2=None, op0=ALU.mult),
              r=["a_neg"], w=["a_neg"])

            ck(1)
            def load_w(w_ap, k0, kn, c0, cn):
                i = wstate["i"]; wstate["i"] = (i + 1) % 2
                assert kn * cn <= WB
                view = wbufs[i][:, 0:kn * cn].rearrange("p (k c) -> p k c", k=kn)
                src = w_ap[k0 * 128:(k0 + kn) * 128, c0:c0 + cn].rearrange("(k p) c -> p k c", p=128)
                if i == 1:
                    A("pool", lambda e: e.dma_start(out=view, in_=src), w=["wbuf1"], dma=True)
                else:
                    j = wstate.get("j", 0); wstate["j"] = (j + 1) % 2
                    sview = wstg[j][:, 0:kn * cn].rearrange("p (k c) -> p k c", k=kn)
                    A("sp", lambda e: e.dma_start(out=sview, in_=src), w=["wstg%d" % j], dma=True)
                    A("act", lambda e: e.copy(out=wbufs[0][:, 0:kn * cn], in_=wstg[j][:, 0:kn * cn]), r=["wstg%d" % j], w=["wbuf0"])
                return view, "wbuf%d" % i

            def psbf(b):
                return ps[b][:].bitcast(BF16)

            ss_t = sb("ss_t", [128, 1]); rt_t = sb("rt_t", [128, 1]); rstd_t = sb("rstd_t", [128, 1])
            xn_t = sb("xn_t", [128, max(D, 2048)], BF16)

            def norm_T(x_tile, xkey, nrows, gname, out_fn, okey):
                A("act", lambda e: e.activation(out=xn_t[0:nrows, 0:D], in_=x_tile, func=AF.Square, accum_out=ss_t[0:nrows, :]),
                  r=[xkey], w=["xn_t", "ss_t"])
                A("act", lambda e: e.activation(out=rt_t[0:nrows, :], in_=ss_t[0:nrows, :], func=AF.Sqrt, scale=1.0 / D,
                                                bias=epsb[0:nrows, :]), r=["ss_t", "epsb"], w=["rt_t"])
                A("dve", lambda e: e.reciprocal(out=rstd_t[0:nrows, :], in_=rt_t[0:nrows, :]), r=["rt_t"], w=["rstd_t"])
                A("dve", lambda e: e.tensor_scalar(out=xn_t[0:nrows, 0:D], in0=x_tile, scalar1=rstd_t[0:nrows, 0:1], scalar2=None,
                                                   op0=ALU.mult), r=[xkey, "rstd_t"], w=["xn_t"])
                for k0 in range(0, KC, 8):
                    kn = min(8, KC - k0)
                    b = bank()
                    for j in range(kn):
                        kc = k0 + j
                        A("pe", lambda e, kc=kc, j=j, b=b: e.transpose(out=psbf(b)[:, j * 128:j * 128 + nrows],
                                                                      in_=xn_t[0:nrows, kc * 128:(kc + 1) * 128],
                                                                      identity=ident_bf[0:nrows, 0:nrows]),
                          r=["xn_t", "ident_bf"], w=["ps%d" % b])
                    for j in range(kn):
                        kc = k0 + j
                        A("act", lambda e, kc=kc, j=j, b=b: e.activation(out=out_fn(kc), in_=psbf(b)[:, j * 128:j * 128 + nrows],
                                                                        func=AF.Identity, scale=pcol(gname, kc)),
                          r=["ps%d" % b, "prm"], w=[okey])

            def linear_fm(w_ap, KN, c0, ncols, rhs_fn, rkeys, N, evac):
                ks = max(1, WB // 512)
                for s0 in range(0, ncols, 512):
                    sw = min(512, ncols - s0)
                    chunks = [(o0, min(128, sw - o0)) for o0 in range(0, sw, 128)]
                    banks = [bank() for _ in chunks]
                    for k0 in range(0, KN, ks):
                        kn = min(ks, KN - k0)
                        view, wkey = load_w(w_ap, k0, kn, c0 + s0, sw)
                        for (o0, m), b in zip(chunks, banks):
                            for kk in range(kn):
                                kc = k0 + kk
                                A("pe", lambda e, kc=kc, kk=kk, o0=o0, m=m, b=b, view=view: e.matmul(
                                    ps[b][0:m, 0:N], lhsT=view[:, kk, o0:o0 + m], rhs=rhs_fn(kc), start=(kc == 0), stop=(kc == KN - 1)),
                                  r=[wkey] + rkeys, w=["ps%d" % b])
                    for (o0, m), b in zip(chunks, banks):
                        evac(c0 + s0 + o0, m, b)

            def linear_tm(w_ap, KN, ncols, cg, lhsT_fn, lkeys, tiles, evac):
                ks = max(1, WB // cg)
                for c0 in range(0, ncols, cg):
                    cn = min(cg, ncols - c0)
                    banks = {tid: bank() for (tid, _) in tiles}
                    for k0 in range(0, KN, ks):
                        kn = min(ks, KN - k0)
                        view, wkey = load_w(w_ap, k0, kn, c0, cn)
                        for (tid, ntok) in tiles:
                            b = banks[tid]
                            for kk in range(kn):
                                kc = k0 + kk
                                A("pe", lambda e, kc=kc, kk=kk, tid=tid, ntok=ntok, b=b, view=view, cn=cn: e.matmul(
                                    ps[b][0:ntok, 0:cn], lhsT=lhsT_fn(kc, tid), rhs=view[:, kk, 0:cn], start=(kc == 0), stop=(kc == KN - 1)),
                                  r=[wkey] + lkeys, w=["ps%d" % b])
                    for (tid, ntok) in tiles:
                        evac(tid, ntok, c0, cn, banks[tid])

            sq_t = sb("sq_t", [128, 256], BF16); nrm_r = sb("nrm_r", [128, 256]); nrm_s = sb("nrm_s", [128, 256])

            def headnorm_fm(src_fn, skey, nchunks, N, ones_ap, okey_ones, dim, gname, out_bf_fn, obkey, out_f_fn=None, ofkey=None):
                for c in range(nchunks):
                    b = bank()
                    A("act", lambda e, c=c: e.activation(out=sq_t[:, 0:N], in_=src_fn(c), func=AF.Square), r=[skey], w=["sq_t"])
                    A("pe", lambda e, b=b: e.matmul(ps[b][:, 0:N], lhsT=ones_ap, rhs=sq_t[:, 0:N], start=True, stop=True),
                      r=["sq_t", okey_ones], w=["ps%d" % b])
                    A("act", lambda e, b=b: e.activation(out=nrm_s[:, 0:N], in_=ps[b][:, 0:N], func=AF.Sqrt, scale=1.0 / dim, bias=epsb[:]),
                      r=["ps%d" % b, "epsb"], w=["nrm_s"])
                    A("dve", lambda e: e.reciprocal(out=nrm_r[:, 0:N], in_=nrm_s[:, 0:N]), r=["nrm_s"], w=["nrm_r"])
                    A("dve", lambda e, c=c: e.scalar_tensor_tensor(out=out_bf_fn(c), in0=src_fn(c), scalar=pcol(gname), in1=nrm_r[:, 0:N],
                                                                   op0=ALU.mult, op1=ALU.mult), r=[skey, "nrm_r", "prm"], w=[obkey])
                    if out_f_fn is not None:
                        A("dve", lambda e, c=c: e.scalar_tensor_tensor(out=out_f_fn(c), in0=src_fn(c), scalar=pcol(gname), in1=nrm_r[:, 0:N],
                                                                       op0=ALU.mult, op1=ALU.mult), r=[skey, "nrm_r", "prm"], w=[ofkey])

            xt = [sb("xt%d" % i, [128, D]) for i in range(NTG)]
            stg = sb("stg", [128, 2048])
            hT = sb("hT", [128, KC, NG], BF16)
            qf = sb("qf", [128, 8, NG], BF16)
            RR = sb("RR", [128, 48 * NG], BF16)
            qn = RR[:, 40 * NG:48 * NG].rearrange("p (c n) -> p c n", c=8)
            kf = sb("kf", [128, 4, NG], BF16); kn_bf = sb("kn_bf", [128, 4, NG], BF16); knf = sb("knf", [128, 4, NG])
            vTf = sb("vTf", [128, 2, NG])
            zs = RR[:, 0:16 * NG].rearrange("p (c n) -> p c n", c=16)
            xc = RR[:, 16 * NG:40 * NG].rearrange("p (c n) -> p c n", c=24)
            dtT = sb("dtT", [32, NG]); dtaT = sb("dtaT", [32, NG])
            mixT = sb("mixT", [128, 24, NG], BF16)
            ctmp = sb("ctmp", [128, 3 + NG]); cacc = sb("cacc", [128, NG])
            convc = sb("convc", [128, 24, 3]); uc = sb("uc", [128, FC2, 2])
            kd_prev = sb("kd_prev", [128, 4, 128], BF16); vd_prev = sb("vd_prev", [128, 4, 2, 64], BF16)
            vd_cur = sb("vd_cur", [128, 4, 2, 64], BF16)
            hst_f = sb("hst_f", [128, D_SSD]); hst_bf = sb("hst_bf", [128, D_SSD], BF16)
            memkT = sb("memkT", [128, 4, NMEM], BF16); memv = sb("memv", [128, MB, D_X], BF16)
            assert FC <= 48
            gT = RR[:, 0:FC * NG].rearrange("p (c n) -> p c n", c=FC)
            qcf = sb("qcf", [128, 4, NG], BF16); qcn = sb("qcn", [128, 4, NG], BF16); ocT = sb("ocT", [128, 4, NG], BF16)

            pT = [sb("pT%d" % i, [128, 512], BF16) for i in range(2)]
            rden = sb("rden", [128, 512])

            def attn_tile(q_fn, qkey, nq, kcur_fn, kckey, vcur, vckey, ncur, kprev_fn, kpkey, vprev, vpkey, nprev, out_fn, okey, masks):
                for kvh in range(4):
                    blocks = []
                    if nprev:
                        blocks.append((kprev_fn, kpkey, vprev, vpkey, nprev, mprev4, "mprev4"))
                    blocks.append((kcur_fn, kckey, vcur, vckey, ncur, mcur4, "mcur4"))
                    pts = []
                    for bi, (kfn, kkey, vv, vkey, ns, msk, mkey) in enumerate(blocks):
                        bAB = (bank(), bank())
                        for g in range(4):
                            h = kvh * 4 + g
                            c, half = h // 2, h % 2
                            gg = g // 2
                            b = bAB[half]
                            A("pe", lambda e, gg=gg, c=c, half=half, b=b, kfn=kfn, ns=ns, kvh=kvh: e.matmul(
                                ps[b][0:ns, gg * 128:gg * 128 + nq], lhsT=kfn(kvh)[half * 64:(half + 1) * 64, 0:ns],
                                rhs=q_fn(c)[half * 64:(half + 1) * 64, 0:nq], start=True, stop=True),
                              r=[kkey, qkey], w=["ps%d" % b])
                        pt = pT[bi]
                        ptv = pt[0:ns, :].rearrange("p (g q) -> p g q", g=4)[:, :, 0:nq]
                        for half in range(2):
                            b = bAB[half]
                            psv = ps[b][0:ns, 0:256].rearrange("p (g q) -> p g q", g=2)[:, :, 0:nq]
                            A("act", lambda e, ptv=ptv, psv=psv, half=half: e.activation(out=ptv[:, half * 2:half * 2 + 2, :], in_=psv, func=AF.Exp, scale=HD ** -0.5),
                              r=["ps%d" % b], w=["pT%d" % bi])
                        if masks:
                            A("dve", lambda e, ptv=ptv, msk=msk, ns=ns: e.tensor_tensor(out=ptv, in0=ptv, in1=msk[0:ns, :, 0:nq], op=ALU.mult),
                              r=["pT%d" % bi, mkey], w=["pT%d" % bi])
                        pts.append((ptv, "pT%d" % bi, vv, vkey, ns))
                    ck(41)
                    bo, bd = bank(), bank()
                    for bi, (ptv, pkey, vv, vkey, ns) in enumerate(pts):
                        A("pe", lambda e, ptv=ptv, vv=vv, ns=ns, bi=bi, bo=bo, kvh=kvh, npts=len(pts): e.matmul(
                            ps[bo][:, 0:4 * nq].rearrange("p (g q) -> p g q", g=4), lhsT=vv[0:ns, kvh, :, :].rearrange("p a d -> p (a d)"),
                            rhs=ptv, start=(bi == 0), stop=(bi == npts - 1)), r=[pkey, vkey], w=["ps%d" % bo])
                        A("pe", lambda e, ptv=ptv, ns=ns, bi=bi, bd=bd, npts=len(pts): e.matmul(
                            ps[bd][:, 0:4 * nq].rearrange("p (g q) -> p g q", g=4), lhsT=ones_bf[0:ns, :],
                            rhs=ptv, start=(bi == 0), stop=(bi == npts - 1)), r=[pkey, "ones_bf"], w=["ps%d" % bd])
                    ck(42)
                    for g in range(4):
                        h = kvh * 4 + g
                        gi = (g % 2) * 2 + g // 2
                        A("dve", lambda e, gi=gi, h=h, bd=bd: e.tensor_scalar(out=rden[:, gi * nq:(gi + 1) * nq], in0=ps[bd][:, gi * nq:(gi + 1) * nq],
                                                                              scalar1=esink[:, h:h + 1], scalar2=None, op0=ALU.add),
                          r=["ps%d" % bd, "esink"], w=["rden"])
                    A("dve", lambda e: e.reciprocal(out=rden[:, 0:4 * nq], in_=rden[:, 0:4 * nq]), r=["rden"], w=["rden"])
                    for g in range(4):
                        h = kvh * 4 + g
                        c, half = h // 2, h % 2
                        gi = (g % 2) * 2 + g // 2
                        A("dve", lambda e, gi=gi, c=c, half=half, bo=bo: e.tensor_tensor(
                            out=out_fn(c, half), in0=ps[bo][half * 64:(half + 1) * 64, gi * nq:(gi + 1) * nq],
                            in1=rden[half * 64:(half + 1) * 64, gi * nq:(gi + 1) * nq], op=ALU.mult),
                          r=["ps%d" % bo, "rden"], w=[okey])

            xdt = sb("xdt", [128, 32, 64], BF16); xdd = sb("xdd", [128, 32, 64], BF16)
            b_tok = sb("b_tok", [128, 4, 128], BF16)
            dt_tok = sb("dt_tok", [128, 32]); dta_tok = sb("dta_tok", [128, 32]); dta_exp = sb("dta_exp", [128, 4, 64])
            la_t = sb("la_t", [128, 32]); dec_end = sb("dec_end", [128, 32]); cd_bc = sb("cd_bc", [128, 32])
            cbm = sb("cbm", [128, 4, 128])
            dU = [sb("dU%d" % i, [128, 128]) for i in range(2)]
            eseg = sb("eseg", [128, 4, 128]); MT = [sb("MT%d" % i, [128, 4, 128], BF16) for i in range(2)]
            ela = eseg[:, 2:4, :]; ytmp = eseg[:, 0:2, :]
            ybuf = sb("ybuf", [128, 16, 128]); ysq = xn_t[:, 0:2048].rearrange("p (j l) -> p j l", j=16)

            def ssd_tile(L, xc_fn, xkey, dt_ap, dta_ap, dkey, zs_fn, zkey, out_fn, okey):
                xs_bks = []
                for j0 in range(0, 16, 8):
                    b = bank()
                    for j in range(8):
                        A("pe", lambda e, j=j, j0=j0, b=b: e.transpose(out=psbf(b)[0:L, j * 128:(j + 1) * 128], in_=xc_fn(j0 + j), identity=ident_bf[:, :]),
                          r=[xkey, "ident_bf"], w=["ps%d" % b])
                    xs_bks.append((j0, b))
                b = bank()
                for g in range(4):
                    A("pe", lambda e, g=g, b=b: e.transpose(out=psbf(b)[0:L, g * 128:(g + 1) * 128], in_=xc_fn(16 + g), identity=ident_bf[:, :]),
                      r=[xkey, "ident_bf"], w=["ps%d" % b])
                A("act", lambda e, b=b: e.copy(out=b_tok[0:L, :, :].rearrange("p g n -> p (g n)"), in_=psbf(b)[0:L, 0:512]), r=["ps%d" % b], w=["b_tok"])
                b = bank()
                A("pe", lambda e, b=b: e.transpose(out=ps[b][0:L, 0:32], in_=dt_ap, identity=ident_f[0:32, 0:32]), r=[dkey, "cst"], w=["ps%d" % b])
                A("pe", lambda e, b=b: e.transpose(out=ps[b][0:L, 32:64], in_=dta_ap, identity=ident_f[0:32, 0:32]), r=[dkey, "cst"], w=["ps%d" % b])
                A("act", lambda e, b=b: e.copy(out=dt_tok[0:L, :], in_=ps[b][0:L, 0:32]), r=["ps%d" % b], w=["dt_tok"])
                A("act", lambda e, b=b: e.copy(out=dta_tok[0:L, :], in_=ps[b][0:L, 32:64]), r=["ps%d" % b], w=["dta_tok"])
                ck(50)
                b = bank()
                A("pe", lambda e, b=b: e.matmul(ps[b][0:L, 0:32], lhsT=tri_f[0:L, 0:L], rhs=dta_tok[0:L, :], start=True, stop=True),
                  r=["cst", "dta_tok"], w=["ps%d" % b])
                A("pe", lambda e, b=b: e.matmul(ps[b][0:128, 32:64], lhsT=ones_f[0:L, :], rhs=dta_tok[0:L, :], start=True, stop=True),
                  r=["ones_f", "dta_tok"], w=["ps%d" % b])
                A("act", lambda e, b=b: e.copy(out=la_t[0:L, :], in_=ps[b][0:L, 0:32]), r=["ps%d" % b], w=["la_t"])
                A("dve", lambda e, b=b: e.tensor_tensor(out=dec_end[0:L, :], in0=ps[b][0:L, 32:64], in1=la_t[0:L, :], op=ALU.subtract),
                  r=["ps%d" % b, "la_t"], w=["dec_end"])
                A("act", lambda e: e.activation(out=dec_end[0:L, :], in_=dec_end[0:L, :], func=AF.Exp), r=["dec_end"], w=["dec_end"])
                A("act", lambda e, b=b: e.activation(out=cd_bc[:, :], in_=ps[b][:, 32:64], func=AF.Exp), r=["ps%d" % b], w=["cd_bc"])
                ck(51)
                for (j0, bx) in xs_bks:
                    A("dve", lambda e, j0=j0, bx=bx: e.tensor_tensor(out=xdt[0:L, j0 * 2:(j0 + 8) * 2, :],
                                                                     in0=psbf(bx)[0:L, 0:1024].rearrange("p (h d) -> p h d", h=16),
                                                                     in1=dt_tok[0:L, j0 * 2:(j0 + 8) * 2].unsqueeze(2).broadcast_to([L, 16, 64]), op=ALU.mult),
                      r=["ps%d" % bx, "dt_tok"], w=["xdt"])
                A("dve", lambda e: e.tensor_tensor(out=xdd[0:L], in0=xdt[0:L], in1=dec_end[0:L, :].unsqueeze(2).broadcast_to([L, 32, 64]), op=ALU.mult),
                  r=["xdt", "dec_end"], w=["xdd"])
                ck(52)
                b = bank()
                for g in range(4):
                    A("pe", lambda e, g=g, b=b: e.matmul(ps[b][0:L, g * 128:g * 128 + L], lhsT=xc_fn(16 + g), rhs=xc_fn(20 + g), start=True, stop=True),
                      r=[xkey], w=["ps%d" % b])
                A("dve", lambda e, b=b: e.tensor_tensor(out=cbm[0:L, :, 0:L], in0=ps[b][0:L, :].rearrange("p (g l) -> p g l", g=4)[:, :, 0:L],
                                                        in1=mcur4[0:L, :, 0:L], op=ALU.mult), r=["ps%d" % b, "mcur4"], w=["cbm"])
                ck(53)
                for q4 in range(8):
                    b = bank()
                    for i in range(4):
                        h = q4 * 4 + i
                        du = dU[h % 2]
                        A("dve", lambda e, h=h, du=du: e.tensor_scalar(out=du[0:L, 0:L], in0=U_f[0:L, 0:L], scalar1=dta_tok[0:L, h:h + 1], scalar2=None, op0=ALU.mult),
                          r=["cst", "dta_tok"], w=["dU%d" % (h % 2)])
                        A("pe", lambda e, i=i, du=du, b=b: e.matmul(ps[b][0:L, i * 128:i * 128 + L], lhsT=du[0:L, 0:L], rhs=tri_f[0:L, 0:L], start=True, stop=True),
                          r=["dU%d" % (h % 2), "cst"], w=["ps%d" % b])
                    ck(54)
                    A("act", lambda e, b=b: e.activation(out=eseg[0:L, :, 0:L], in_=ps[b][0:L, :].rearrange("p (g l) -> p g l", g=4)[:, :, 0:L], func=AF.Exp),
                      r=["ps%d" % b], w=["eseg"])
                    g = q4 // 2
                    mt = MT[q4 % 2]
                    A("dve", lambda e, g=g, mt=mt: e.tensor_tensor(out=mt[0:L, :, 0:L], in0=eseg[0:L, :, 0:L],
                                                                   in1=cbm[0:L, g:g + 1, 0:L].broadcast_to([L, 4, L]), op=ALU.mult),
                      r=["eseg", "cbm"], w=["MT%d" % (q4 % 2)])
                    ck(55)
                    A("dve", lambda e, q4=q4: e.tensor_copy(out=dta_exp[0:L], in_=dta_tok[0:L, q4 * 4:(q4 + 1) * 4].unsqueeze(2).broadcast_to([L, 4, 64])),
                      r=["dta_tok"], w=["dta_exp"])
                    by, bi_, be = bank(), bank(), bank()
                    for jj in range(2):
                        j = q4 * 2 + jj
                        for h2 in range(2):
                            i = jj * 2 + h2
                            A("pe", lambda e, j=j, h2=h2, i=i, jj=jj, mt=mt, by=by: e.matmul(
                                ps[by][h2 * 64:(h2 + 1) * 64, jj * 128:jj * 128 + L], lhsT=xdt[0:L, 2 * j + h2, :], rhs=mt[0:L, i, 0:L], start=True, stop=True),
                              r=["xdt", "MT%d" % (q4 % 2)], w=["ps%d" % by])
                        A("pe", lambda e, j=j, jj=jj, g=g, bi_=bi_: e.matmul(ps[bi_][:, jj * 128:jj * 128 + L], lhsT=hst_bf[:, j * 128:(j + 1) * 128], rhs=xc_fn(20 + g),
                                                                          start=True, stop=True), r=["hst_bf", xkey], w=["ps%d" % bi_])
                        A("pe", lambda e, j=j, jj=jj, be=be: e.matmul(ps[be][:, jj * 128:jj * 128 + L], lhsT=dta_exp[0:L, 2 * jj:2 * jj + 2, :].rearrange("p h d -> p (h d)"),
                                                                      rhs=tri_f[0:L, 0:L], start=True, stop=True), r=["dta_exp", "cst"], w=["ps%d" % be])
                    ck(56)
                    A("act", lambda e, be=be: e.activation(out=ela[:, 0:2, 0:L], in_=ps[be][:, 0:256].rearrange("p (g l) -> p g l", g=2)[:, :, 0:L], func=AF.Exp),
                      r=["ps%d" % be], w=["eseg"])
                    A("dve", lambda e, bi_=bi_: e.tensor_tensor(out=ytmp[:, 0:2, 0:L], in0=ps[bi_][:, 0:256].rearrange("p (g l) -> p g l", g=2)[:, :, 0:L],
                                                                in1=ela[:, 0:2, 0:L], op=ALU.mult), r=["ps%d" % bi_, "eseg"], w=["eseg"])
                    A("dve", lambda e, by=by: e.tensor_tensor(out=ytmp[:, 0:2, 0:L], in0=ps[by][:, 0:256].rearrange("p (g l) -> p g l", g=2)[:, :, 0:L],
                                                              in1=ytmp[:, 0:2, 0:L], op=ALU.add), r=["ps%d" % by, "eseg"], w=["eseg"])
                    for jj in range(2):
                        j = q4 * 2 + jj
                        A("dve", lambda e, j=j, jj=jj: e.scalar_tensor_tensor(out=ybuf[:, j, 0:L], in0=xc_fn(j), scalar=pcol("d_skip", j), in1=ytmp[:, jj, 0:L],
                                                                              op0=ALU.mult, op1=ALU.add), r=[xkey, "eseg", "prm"], w=["ybuf"])
                ck(57)
                for g in range(4):
                    b = bank()
                    A("pe", lambda e, g=g, b=b: e.matmul(ps[b][:, 0:512], lhsT=b_tok[0:L, g, :], rhs=xdd[0:L, g * 8:(g + 1) * 8, :].rearrange("p h d -> p (h d)"),
                                                         start=True, stop=True), r=["b_tok", "xdd"], w=["ps%d" % b])
                    ck(570)
                    hv = hst_f[:, g * 512:(g + 1) * 512].rearrange("p (h d) -> p h d", h=8)
                    A("dve", lambda e, g=g, hv=hv: e.tensor_tensor(out=hv, in0=hv, in1=cd_bc[:, g * 8:(g + 1) * 8].unsqueeze(2).broadcast_to([128, 8, 64]), op=ALU.mult),
                      r=["hst_f", "cd_bc"], w=["hst_f"])
                    ck(571)
                    A("dve", lambda e, g=g, b=b: e.tensor_tensor(out=hst_f[:, g * 512:(g + 1) * 512], in0=ps[b][:, 0:512], in1=hst_f[:, g * 512:(g + 1) * 512], op=ALU.add),
                      r=["ps%d" % b, "hst_f"], w=["hst_f"])
                    ck(572)
                ck(573)
                A("dve", lambda e: e.tensor_copy(out=hst_bf[:], in_=hst_f[:]), r=["hst_f", "hst_bf"], w=["hst_bf"])
                ck(58)
                ybv = ybuf[:, :, 0:L]
                A("dve", lambda e: e.tensor_tensor(out=ybv, in0=ybv, in1=zs_fn(), op=ALU.mult), r=["ybuf", zkey], w=["ybuf"])
                A("act", lambda e: e.activation(out=ysq[:, :, 0:L], in_=ybv, func=AF.Square), r=["ybuf"], w=["xn_t"])
                b = bank()
                for j in range(16):
                    A("pe", lambda e, j=j, b=b: e.matmul(ps[b][:, 0:L], lhsT=ones_bf[:, :], rhs=ysq[:, j, 0:L], start=(j == 0), stop=(j == 15)),
                      r=["xn_t", "ones_bf"], w=["ps%d" % b])
                A("act", lambda e, b=b: e.activation(out=nrm_s[:, 0:L], in_=ps[b][:, 0:L], func=AF.Sqrt, scale=1.0 / D_SSD, bias=epsb[:]),
                  r=["ps%d" % b, "epsb"], w=["nrm_s"])
                A("dve", lambda e: e.reciprocal(out=nrm_r[:, 0:L], in_=nrm_s[:, 0:L]), r=["nrm_s"], w=["nrm_r"])
                for j in range(16):
                    A("dve", lambda e, j=j: e.scalar_tensor_tensor(out=out_fn(j), in0=ybuf[:, j, 0:L], scalar=pcol("ssd_norm_g", j), in1=nrm_r[:, 0:L],
                                                                   op0=ALU.mult, op1=ALU.mult), r=["ybuf", "nrm_r", "prm"], w=[okey])

            pc = [sb("pc%d" % i, [128, 512], BF16) for i in range(2)]

            def cross_attn(N, kT, kkey, vv, vkey):
                for h in range(4):
                    pts = []
                    for mb in range(MB):
                        b = bank()
                        A("pe", lambda e, h=h, mb=mb, b=b: e.matmul(ps[b][:, 0:N], lhsT=kT[:, h, mb * 128:(mb + 1) * 128], rhs=qcn[:, h, 0:N], start=True, stop=True),
                          r=[kkey, "qcn"], w=["ps%d" % b])
                        A("act", lambda e, mb=mb, b=b: e.activation(out=pc[mb % 2][:, 0:N], in_=ps[b][:, 0:N], func=AF.Exp, scale=MEM_D ** -0.5),
                          r=["ps%d" % b], w=["pc%d" % (mb % 2)])
                        pts.append(mb)
                        if mb % 2 == 1 or mb == MB - 1:
                            pass
                    bo, bd = bank(), bank()
                    for mb in range(MB):
                        A("pe", lambda e, h=h, mb=mb, bo=bo: e.matmul(ps[bo][:, 0:N], lhsT=vv[:, mb, h * 128:(h + 1) * 128], rhs=pc[mb % 2][:, 0:N],
                                                                     start=(mb == 0), stop=(mb == MB - 1)), r=[vkey, "pc%d" % (mb % 2)], w=["ps%d" % bo])
                        A("pe", lambda e, mb=mb, bd=bd: e.matmul(ps[bd][:, 0:N], lhsT=ones_bf[:, :], rhs=pc[mb % 2][:, 0:N],
                                                                start=(mb == 0), stop=(mb == MB - 1)), r=["ones_bf", "pc%d" % (mb % 2)], w=["ps%d" % bd])
                    A("dve", lambda e, bd=bd: e.reciprocal(out=nrm_r[:, 0:N], in_=ps[bd][:, 0:N]), r=["ps%d" % bd], w=["nrm_r"])
                    A("dve", lambda e, h=h, bo=bo: e.tensor_tensor(out=ocT[:, h, 0:N], in0=ps[bo][:, 0:N], in1=nrm_r[:, 0:N], op=ALU.mult),
                      r=["ps%d" % bo, "nrm_r"], w=["ocT"])

            def conv_fm(b, m, N, taps, wname, bname, ch, carry_ap, ckey, out_ap, okey, func, out2=None, o2key=None):
                H = taps - 1
                A("act", lambda e: e.copy(out=ctmp[0:m, H:H + N], in_=ps[b][0:m, 0:N]), r=["ps%d" % b], w=["ctmp"])
                A("dve", lambda e: e.tensor_copy(out=ctmp[0:m, 0:H], in_=carry_ap), r=[ckey], w=["ctmp"])
                A("dve", lambda e: e.tensor_copy(out=carry_ap, in_=ctmp[0:m, N:N + H]), r=["ctmp"], w=[ckey])
                wo = cfg.po[wname][0] + ch * taps
                A("dve", lambda e: e.tensor_scalar(out=cacc[0:m, 0:N], in0=ctmp[0:m, H:H + N], scalar1=prm[0:m, wo + H:wo + H + 1],
                                                   scalar2=pcol(bname, ch)[0:m, :], op0=ALU.mult, op1=ALU.add), r=["ctmp", "prm"], w=["cacc"])
                for k in range(H):
                    A("dve", lambda e, k=k: e.scalar_tensor_tensor(out=cacc[0:m, 0:N], in0=ctmp[0:m, k:k + N], scalar=prm[0:m, wo + k:wo + k + 1],
                                                                   in1=cacc[0:m, 0:N], op0=ALU.mult, op1=ALU.add), r=["ctmp", "cacc", "prm"], w=["cacc"])
                if out_ap is None:
                    pass
                elif func is None:
                    A("act", lambda e: e.copy(out=out_ap, in_=cacc[0:m, 0:N]), r=["cacc"], w=[okey])
                else:
                    A("act", lambda e: e.activation(out=out_ap, in_=cacc[0:m, 0:N], func=func), r=["cacc"], w=[okey])
                if out2 is not None:
                    A("act", lambda e: e.activation(out=out2, in_=cacc[0:m, 0:N], func=func), r=["cacc"], w=[o2key])

            def rows_out(src_fn, skey, nch, n, dst_fn):
                for c0 in range(0, nch, 16):
                    cn = min(16, nch - c0)
                    for i0 in range(0, cn, 4):
                        b = bank()
                        n4 = min(4, cn - i0)
                        for i in range(n4):
                            A("pe", lambda e, c0=c0, i0=i0, i=i, b=b: e.transpose(out=ps[b][0:n, i * 128:(i + 1) * 128], in_=src_fn(c0 + i0 + i), identity=ident_f),
                              r=[skey, "cst"], w=["ps%d" % b])
                        A("act", lambda e, i0=i0, n4=n4, b=b: e.copy(out=stg[0:n, i0 * 128:(i0 + n4) * 128], in_=ps[b][0:n, 0:n4 * 128]), r=["ps%d" % b], w=["stg"])
                    A("sp", lambda e, c0=c0, cn=cn: e.dma_start(out=dst_fn(c0, cn), in_=stg[0:n, 0:cn * 128]), r=["stg"], dma=True)

            otok = sb("otok", [128, 512])
            for mb in range(MB):
                A("sp", lambda e, mb=mb: e.dma_start(out=xt[0][:], in_=memp[mb * 128:(mb + 1) * 128, :]), w=["xt0"], dma=True)
                norm_T(xt[0][:], "xt0", 128, "g_mem", lambda kc: hT[:, kc, 0:128], "hT")
                ck(11)

                def ev_mk(c, m, b):
                    A("act", lambda e: e.copy(out=qcf[:, c // 128, 0:128], in_=ps[b][:, 0:128]), r=["ps%d" % b], w=["qcf"])
                linear_fm(w_ck, KC, 0, D_X, lambda kc: hT[:, kc, 0:128], ["hT"], 128, ev_mk)
                ck(12)
                headnorm_fm(lambda c: qcf[:, c, 0:128], "qcf", 4, 128, ones_bf[:, :], "ones_bf", MEM_D, "ck_norm_g",
                            lambda c, mb=mb: memkT[:, c, mb * 128:(mb + 1) * 128], "memkT", lambda c: knf[:, c, 0:128], "knf")
                b = bank()
                for h in range(4):
                    A("pe", lambda e, h=h, b=b: e.transpose(out=ps[b][:, h * 128:(h + 1) * 128], in_=knf[:, h, 0:128], identity=ident_f),
                      r=["knf", "cst"], w=["ps%d" % b])
                A("act", lambda e, b=b: e.copy(out=otok[:], in_=ps[b][:, :]), r=["ps%d" % b], w=["otok"])
                A("sp", lambda e, mb=mb: e.dma_start(out=o_mk[mb * 128:(mb + 1) * 128, :], in_=otok[:]), r=["otok"], dma=True)
                ck(14)

                def ev_mv(tid, ntok, c0, cn, b, mb=mb):
                    A("act", lambda e: e.copy(out=otok[:, c0:c0 + cn], in_=ps[b][:, 0:cn]), r=["ps%d" % b], w=["otok"])
                    A("dve", lambda e: e.tensor_copy(out=memv[:, mb, c0:c0 + cn], in_=ps[b][:, 0:cn]), r=["ps%d" % b], w=["memv"])
                    A("sp", lambda e: e.dma_start(out=o_mv[mb * 128:(mb + 1) * 128, c0:c0 + cn], in_=otok[:, c0:c0 + cn]), r=["otok"], dma=True)
                linear_tm(w_cv, KC, D_X, 512, lambda kc, tid: hT[:, kc, 0:128], ["hT"], [(0, 128)], ev_mv)
                ck(15)

            ck(2)
            def stage_inproj(N, conv_mode):
                def ev(c, m, b):
                    if c < C_K:
                        A("act", lambda e: e.copy(out=qf[:, c // 128, 0:N], in_=ps[b][:, 0:N]), r=["ps%d" % b], w=["qf"])
                    elif c < C_V:
                        ch = (c - C_K) // 128
                        for half in range(2):
                            kvh = ch * 2 + half
                            for dst in range(2):
                                eng = "act" if dst == 0 else "dve"
                                if eng == "act":
                                    A("act", lambda e, kvh=kvh, half=half, dst=dst: e.copy(out=kf[dst * 64:(dst + 1) * 64, kvh, 0:N],
                                                                                          in_=ps[b][half * 64:(half + 1) * 64, 0:N]), r=["ps%d" % b], w=["kf"])
                                else:
                                    A("dve", lambda e, kvh=kvh, half=half, dst=dst: e.tensor_copy(out=kf[dst * 64:(dst + 1) * 64, kvh, 0:N],
                                                                                                 in_=ps[b][half * 64:(half + 1) * 64, 0:N]), r=["ps%d" % b], w=["kf"])
                    elif c < C_Z:
                        A("act", lambda e: e.copy(out=vTf[:, (c - C_V) // 128, 0:N], in_=ps[b][:, 0:N]), r=["ps%d" % b], w=["vTf"])
                    elif c < C_XBC:
                        A("act", lambda e: e.activation(out=zs[:, (c - C_Z) // 128, 0:N], in_=ps[b][:, 0:N], func=AF.Silu), r=["ps%d" % b], w=["zs"])
                    elif c < C_DT:
                        conv_mode((c - C_XBC) // 128, b)
                    else:
                        A("act", lambda e: e.activation(out=dtT[:, 0:N], in_=ps[b][0:32, 0:N], func=AF.Exp, bias=pcol("dt_bias")[0:32, :]),
                          r=["ps%d" % b, "prm"], w=["dtT"])
                        A("act", lambda e: e.activation(out=dtT[:, 0:N], in_=dtT[:, 0:N], func=AF.Ln, bias=oneb[0:32, :]), r=["dtT", "oneb"], w=["dtT"])
                        A("dve", lambda e: e.tensor_scalar(out=dtaT[:, 0:N], in0=dtT[:, 0:N], scalar1=a_neg[0:32, 0:1], scalar2=None, op0=ALU.mult),
                          r=["dtT", "a_neg"], w=["dtaT"])
                linear_fm(w_in, KC, 0, D_IN, lambda kc: hT[:, kc, 0:N], ["hT"], N, ev)
                headnorm_fm(lambda c: qf[:, c, 0:N], "qf", 8, N, bones_bf[:, :], "bones_bf", HD, "q_norm_g", lambda c: qn[:, c, 0:N], "qn")
                headnorm_fm(lambda c: kf[:, c, 0:N], "kf", 4, N, bones_bf[:, :], "bones_bf", HD, "k_norm_g", lambda c: kn_bf[:, c, 0:N], "kn_bf",
                            lambda c: knf[:, c, 0:N], "knf")

            def stage_outproj(tiles, xtile_fn, xkey_fn):
                def ev(tid, ntok, c0, cn, b):
                    xa = xtile_fn(tid)
                    A("dve", lambda e: e.tensor_tensor(out=xa[0:ntok, c0:c0 + cn], in0=ps[b][0:ntok, 0:cn], in1=xa[0:ntok, c0:c0 + cn], op=ALU.add),
                      r=["ps%d" % b, xkey_fn(tid)], w=[xkey_fn(tid)])
                return ev

            def stage_rest(N, tiles, xtile_fn, xkey_fn, tok_of, cross_fn, ffn_conv_fn):
                ev_res = stage_outproj(tiles, xtile_fn, xkey_fn)
                linear_tm(w_out, 24, D, 512, lambda kc, tid: mixT[:, kc, tok_of(tid):tok_of(tid) + dict(tiles)[tid]], ["mixT"], tiles, ev_res)
                for (tid, ntok) in tiles:
                    norm_T(xtile_fn(tid)[0:ntok, :], xkey_fn(tid), ntok, "g_cross", lambda kc, tid=tid, ntok=ntok: hT[:, kc, tok_of(tid):tok_of(tid) + ntok], "hT")

                def ev_q(c, m, b):
                    A("act", lambda e: e.copy(out=qcf[:, c // 128, 0:N], in_=ps[b][:, 0:N]), r=["ps%d" % b], w=["qcf"])
                linear_fm(w_cq, KC, 0, D_X, lambda kc: hT[:, kc, 0:N], ["hT"], N, ev_q)
                headnorm_fm(lambda c: qcf[:, c, 0:N], "qcf", 4, N, ones_bf[:, :], "ones_bf", MEM_D, "cq_norm_g", lambda c: qcn[:, c, 0:N], "qcn")
                cross_fn()
                linear_tm(w_co, 4, D, 512, lambda kc, tid: ocT[:, kc, tok_of(tid):tok_of(tid) + dict(tiles)[tid]], ["ocT"], tiles, ev_res)
                for (tid, ntok) in tiles:
                    norm_T(xtile_fn(tid)[0:ntok, :], xkey_fn(tid), ntok, "g_ffn", lambda kc, tid=tid, ntok=ntok: hT[:, kc, tok_of(tid):tok_of(tid) + ntok], "hT")

                def ev_up(c, m, b):
                    ffn_conv_fn(c // 128, b)
                linear_fm(w_up, KC, 0, 2 * DFF, lambda kc: hT[:, kc, 0:N], ["hT"], N, ev_up)
                linear_tm(w_down, FC, D, 512, lambda kc, tid: gT[:, kc, tok_of(tid):tok_of(tid) + dict(tiles)[tid]], ["gT"], tiles, ev_res)


            def ffn_post(ch, N):
                if ch < FC:
                    A("act", lambda e: e.activation(out=gT[:, ch, 0:N], in_=cacc[:, 0:N], func=AF.Silu), r=["cacc"], w=["gT"])
                else:
                    A("dve", lambda e: e.tensor_tensor(out=gT[:, ch - FC, 0:N], in0=gT[:, ch - FC, 0:N], in1=cacc[:, 0:N], op=ALU.mult),
                      r=["cacc", "gT"], w=["gT"])

            A("dve", lambda e: e.memset(convc[:], 0.0), w=["convc"])
            A("dve", lambda e: e.memset(uc[:], 0.0), w=["uc"])
            A("dve", lambda e: e.memset(hst_f[:], 0.0), w=["hst_f"])
            A("dve", lambda e: e.memset(hst_bf[:], 0.0), w=["hst_bf"])

            for gi in range(NT // NTG):
                N = NG
                tiles = [(t, 128) for t in range(NTG)]
                for t in range(NTG):
                    gt = gi * NTG + t
                    A("sp", lambda e, t=t, gt=gt: e.dma_start(out=xt[t][:], in_=xp[gt * 128:(gt + 1) * 128, :]), w=["xt%d" % t], dma=True)
                    norm_T(xt[t][:], "xt%d" % t, 128, "g_mix", lambda kc, t=t: hT[:, kc, t * 128:(t + 1) * 128], "hT")

                def conv_ssd(ch, b, N=N):
                    conv_fm(b, 128, N, 4, "ssd_conv_w", "ssd_conv_b", ch, convc[:, ch, :], "convc", xc[:, ch, 0:N], "xc", AF.Silu)
                stage_inproj(N, conv_ssd)
                ck(3)
                for t in range(NTG):
                    gt = gi * NTG + t
                    sl = slice(t * 128, (t + 1) * 128)
                    b = bank()
                    for c2 in range(2):
                        A("pe", lambda e, c2=c2, b=b, sl=sl: e.transpose(out=ps[b][:, c2 * 128:(c2 + 1) * 128], in_=vTf[:, c2, sl], identity=ident_f),
                          r=["vTf", "cst"], w=["ps%d" % b])
                    for dup in range(2):
                        A("act", lambda e, dup=dup, b=b: e.copy(out=vd_cur[:, :, dup, :], in_=ps[b][:, 0:256].rearrange("p (k d) -> p k d", k=4)),
                          r=["ps%d" % b], w=["vd_cur"])
                    if gt == NT - 1:
                        A("act", lambda e, b=b: e.copy(out=otok[:, 0:256], in_=ps[b][:, 0:256]), r=["ps%d" % b], w=["otok"])
                        A("sp", lambda e: e.dma_start(out=o_vp[:, :], in_=otok[:, 0:256]), r=["otok"], dma=True)
                        b2 = bank()
                        for kvh in range(4):
                            A("pe", lambda e, kvh=kvh, b2=b2, sl=sl: e.transpose(out=ps[b2][:, kvh * 128:(kvh + 1) * 128], in_=knf[:, kvh, sl], identity=ident_f),
                              r=["knf", "cst"], w=["ps%d" % b2])
                        A("act", lambda e, b2=b2: e.copy(out=otok[:, 256:512].rearrange("p (k d) -> p k d", k=4),
                                                         in_=ps[b2][:, :].rearrange("p (k d) -> p k d", k=4)[:, :, 0:64]), r=["ps%d" % b2], w=["otok"])
                        A("sp", lambda e: e.dma_start(out=o_kp[:, :], in_=otok[:, 256:512]), r=["otok"], dma=True)
                    ck(40)
                    attn_tile(lambda c, sl=sl: qn[:, c, sl], "qn", 128, lambda kvh, sl=sl: kn_bf[:, kvh, sl], "kn_bf", vd_cur, "vd_cur", 128,
                              lambda kvh: kd_prev[:, kvh, :], "kd_prev", vd_prev, "vd_prev", (128 if gt > 0 else 0),
                              lambda c, half, sl=sl: mixT[half * 64:(half + 1) * 64, c, sl], "mixT", True)
                    A("act", lambda e, sl=sl: e.copy(out=kd_prev[:], in_=kn_bf[:, :, sl]), r=["kn_bf"], w=["kd_prev"])
                    A("act", lambda e: e.copy(out=vd_prev[:], in_=vd_cur[:]), r=["vd_cur"], w=["vd_prev"])
                    ck(4)
                    ssd_tile(128, lambda j, sl=sl: xc[:, j, sl], "xc", dtT[:, sl], dtaT[:, sl], "dtT",
                             lambda sl=sl: zs[:, :, sl], "zs", lambda j, sl=sl: mixT[:, 8 + j, sl], "mixT")
                    ck(5)

                def cross_p(N=N):
                    cross_attn(N, memkT, "memkT", memv, "memv")

                def ffn_conv_p(ch, b, N=N):
                    conv_fm(b, 128, N, 3, "ffn_conv_w", "ffn_conv_b", ch, uc[:, ch, :], "uc", None, None, None)
                    ffn_post(ch, N)
                stage_rest(N, tiles, lambda tid: xt[tid], lambda tid: "xt%d" % tid, lambda tid: tid * 128, cross_p, ffn_conv_p)
                ck(6)
                for t in range(NTG):
                    gt = gi * NTG + t
                    A("sp", lambda e, t=t, gt=gt: e.dma_start(out=yp[gt * 128:(gt + 1) * 128, :], in_=xt[t][:]), r=["xt%d" % t], dma=True)

            for j in range(0, 16, 4):
                b = bank()
                for i in range(4):
                    A("pe", lambda e, j=j, i=i, b=b: e.transpose(out=ps[b][:, i * 128:(i + 1) * 128], in_=hst_f[:, (j + i) * 128:(j + i + 1) * 128], identity=ident_f),
                      r=["hst_f", "cst"], w=["ps%d" % b])
                A("act", lambda e, b=b: e.copy(out=otok[:, :], in_=ps[b][:, :]), r=["ps%d" % b], w=["otok"])
                A("sp", lambda e, j=j: e.dma_start(out=o_sp[j * 128:(j + 4) * 128, :].rearrange("(i p) n -> p i n", p=128),
                                                   in_=otok[:, :].rearrange("p (i n) -> p i n", i=4)), r=["otok"], dma=True)
            rows_out(lambda ch: convc[:, ch, :], "convc", 24, 3, lambda c0, cn: o_cp[:, c0 * 128:(c0 + cn) * 128])
            rows_out(lambda ch: uc[:, ch, :], "uc", FC2, 2, lambda c0, cn: o_fp[:, c0 * 128:(c0 + cn) * 128])

            ck(7)
            N = NS
            A("dve", lambda e: e.memset(xt[0][:], 0.0), w=["xt0"])
            A("sp", lambda e: e.dma_start(out=xt[0][0:NS, :], in_=xs_in[:, :]), w=["xt0"], dma=True)
            norm_T(xt[0][0:NS, :], "xt0", NS, "g_mix", lambda kc: hT[:, kc, 0:NS], "hT")
            pastT = sb("pastT", [128, 24, NS, 3])
            for c0 in range(0, 24, 8):
                A("sp", lambda e, c0=c0: e.dma_start(out=stg[0:NS * 3, 0:1024], in_=sconv_in.rearrange("b k c -> (b k) c")[:, c0 * 128:(c0 + 8) * 128]),
                  w=["stg"], dma=True)
                for i0_ in range(0, 8, 4):
                    b = bank()
                    for i in range(4):
                        A("pe", lambda e, i0_=i0_, i=i, b=b: e.transpose(out=ps[b][:, i * 128:i * 128 + NS * 3], in_=stg[0:NS * 3, (i0_ + i) * 128:(i0_ + i + 1) * 128],
                                                                        identity=ident_f[0:NS * 3, 0:NS * 3]), r=["stg", "cst"], w=["ps%d" % b])
                    A("act", lambda e, c0=c0, i0_=i0_, b=b: e.copy(out=pastT[:, c0 + i0_:c0 + i0_ + 4, :, :].rearrange("p c b k -> p c (b k)"),
                                                               in_=ps[b][:, :].rearrange("p (c x) -> p c x", c=4)[:, :, 0:NS * 3]), r=["ps%d" % b], w=["pastT"])
            xraw = sb("xraw", [128, 24, NS])

            def conv_s(ch, b):
                wo = cfg.po["ssd_conv_w"][0] + ch * 4
                A("act", lambda e: e.copy(out=xraw[:, ch, :], in_=ps[b][:, 0:NS]), r=["ps%d" % b], w=["xraw"])
                A("dve", lambda e: e.tensor_scalar(out=cacc[:, 0:NS], in0=ps[b][:, 0:NS], scalar1=prm[:, wo + 3:wo + 4], scalar2=pcol("ssd_conv_b", ch),
                                                   op0=ALU.mult, op1=ALU.add), r=["ps%d" % b, "prm"], w=["cacc"])
                for k in range(3):
                    A("dve", lambda e, k=k: e.scalar_tensor_tensor(out=cacc[:, 0:NS], in0=pastT[:, ch, :, k], scalar=prm[:, wo + k:wo + k + 1], in1=cacc[:, 0:NS],
                                                                   op0=ALU.mult, op1=ALU.add), r=["pastT", "cacc", "prm"], w=["cacc"])
                A("act", lambda e: e.activation(out=xc[:, ch, 0:NS], in_=cacc[:, 0:NS], func=AF.Silu), r=["cacc"], w=["xc"])
            stage_inproj(NS, conv_s)
            A("sp", lambda e: e.dma_start(out=o_cs[:, 0:2, :], in_=sconv_in[:, 1:3, :]), dma=True)
            rows_out(lambda ch: xraw[:, ch, :], "xraw", 24, NS, lambda c0, cn: o_cs[:, 2, c0 * 128:(c0 + cn) * 128])
            A("sp", lambda e: e.dma_start(out=o_ks[:, 0:127, :], in_=ck_in[:, 1:128, :]), dma=True)
            A("sp", lambda e: e.dma_start(out=o_vs[:, 0:127, :], in_=cv_in[:, 1:128, :]), dma=True)
            A("sp", lambda e: e.dma_start(out=o_fs[:, 0, :], in_=sffn_in[:, 1, :]), dma=True)

            ck(8)
            kst = sb("kst", [128, D_KV]); vst = sb("vst", [128, D_KV]); kdup = sb("kdup", [128, 4, 2, 64], BF16)
            sst = stg[:, :].rearrange("p (j n) -> p j n", j=16); cmk_bf = sb("cmk_bf", [128, MB, D_X], BF16)
            smkT = sb("smkT", [128, 4, NMEM], BF16); smv = sb("smv", [128, MB, D_X], BF16)
            vrow = sb("vrow", [1, 4, 2, 64], BF16); krow_f = sb("krow_f", [1, 256]); vrow_f = sb("vrow_f", [1, 256])
            mixS = sb("mixS", [128, 24, NS], BF16)
            for s in range(NS):
                A("sp", lambda e, s=s: e.dma_start(out=kst[:], in_=ck_in[s, :, :]), w=["kst"], dma=True)
                A("sp", lambda e, s=s: e.dma_start(out=vst[:], in_=cv_in[s, :, :]), w=["vst"], dma=True)
                for dup in range(2):
                    A("dve", lambda e, dup=dup: e.tensor_copy(out=kdup[:, :, dup, :], in_=kst[:, :].rearrange("p (k d) -> p k d", k=4)), r=["kst"], w=["kdup"])
                    A("dve", lambda e, dup=dup: e.tensor_copy(out=vd_prev[:, :, dup, :], in_=vst[:, :].rearrange("p (k d) -> p k d", k=4)), r=["vst"], w=["vd_prev"])
                b = bank()
                for kvh in range(4):
                    A("pe", lambda e, kvh=kvh, b=b: e.transpose(out=psbf(b)[:, kvh * 128:(kvh + 1) * 128], in_=kdup[:, kvh, :, :].rearrange("p a d -> p (a d)"),
                                                                identity=ident_bf[:, :]), r=["kdup", "ident_bf"], w=["ps%d" % b])
                A("act", lambda e, b=b: e.copy(out=kd_prev[:, :, :].rearrange("p k s -> p (k s)"), in_=psbf(b)[:, 0:512]), r=["ps%d" % b], w=["kd_prev"])
                b = bank()
                for c2 in range(2):
                    A("pe", lambda e, c2=c2, b=b, s=s: e.transpose(out=ps[b][0:1, c2 * 128:(c2 + 1) * 128], in_=vTf[:, c2, s:s + 1], identity=ident_f),
                      r=["vTf", "cst"], w=["ps%d" % b])
                for kvh in range(4):
                    A("pe", lambda e, kvh=kvh, b=b, s=s: e.transpose(out=ps[b][0:1, 256 + kvh * 64:256 + (kvh + 1) * 64], in_=knf[0:64, kvh, s:s + 1], identity=ident_f[0:64, 0:64]),
                      r=["knf", "cst"], w=["ps%d" % b])
                for dup in range(2):
                    A("act", lambda e, dup=dup, b=b: e.copy(out=vrow[0:1, :, dup, :], in_=ps[b][0:1, 0:256].rearrange("p (k d) -> p k d", k=4)), r=["ps%d" % b], w=["vrow"])
                A("act", lambda e, b=b: e.copy(out=vrow_f[0:1, :], in_=ps[b][0:1, 0:256]), r=["ps%d" % b], w=["vrow_f"])
                A("act", lambda e, b=b: e.copy(out=krow_f[0:1, 0:256], in_=ps[b][0:1, 256:512]), r=["ps%d" % b], w=["krow_f"])
                A("sp", lambda e, s=s: e.dma_start(out=o_vs[s, 127:128, :], in_=vrow_f[0:1, :]), r=["vrow_f"], dma=True)
                A("sp", lambda e, s=s: e.dma_start(out=o_ks[s, 127:128, :], in_=krow_f[0:1, 0:256]), r=["krow_f"], dma=True)
                attn_tile(lambda c, s=s: qn[:, c, s:s + 1], "qn", 1, lambda kvh, s=s: kn_bf[:, kvh, s:s + 1], "kn_bf", vrow, "vrow", 1,
                          lambda kvh: kd_prev[:, kvh, :], "kd_prev", vd_prev, "vd_prev", 128,
                          lambda c, half, s=s: mixS[half * 64:(half + 1) * 64, c, s:s + 1], "mixS", False)
                A("sp", lambda e, s=s: e.dma_start(out=sst, in_=sst_in[s, :, :].rearrange("(j p) n -> p j n", p=128)), w=["stg"], dma=True)
                for j0 in range(0, 16, 4):
                    b = bank()
                    for i in range(4):
                        A("pe", lambda e, j0=j0, i=i, b=b: e.transpose(out=ps[b][:, i * 128:(i + 1) * 128], in_=sst[:, j0 + i, :], identity=ident_f), r=["stg", "cst"], w=["ps%d" % b])
                    A("act", lambda e, j0=j0, b=b: e.copy(out=hst_f[:, j0 * 128:(j0 + 4) * 128], in_=ps[b][:, :]), r=["ps%d" % b], w=["hst_f"])
                A("dve", lambda e: e.tensor_copy(out=hst_bf[:], in_=hst_f[:]), r=["hst_f"], w=["hst_bf"])
                ssd_tile(1, lambda j, s=s: xc[:, j, s:s + 1], "xc", dtT[:, s:s + 1], dtaT[:, s:s + 1], "dtT",
                         lambda s=s: zs[:, :, s:s + 1], "zs", lambda j, s=s: mixS[:, 8 + j, s:s + 1], "mixS")
                for j0 in range(0, 16, 4):
                    b = bank()
                    for i in range(4):
                        A("pe", lambda e, j0=j0, i=i, b=b: e.transpose(out=ps[b][:, i * 128:(i + 1) * 128], in_=hst_f[:, (j0 + i) * 128:(j0 + i + 1) * 128], identity=ident_f),
                          r=["hst_f", "cst"], w=["ps%d" % b])
                    A("act", lambda e, j0=j0, b=b: e.copy(out=sst[:, j0:j0 + 4, :].rearrange("p j n -> p (j n)"), in_=ps[b][:, :]), r=["ps%d" % b], w=["stg"])
                A("sp", lambda e, s=s: e.dma_start(out=o_ss[s, :, :].rearrange("(j p) n -> p j n", p=128), in_=sst), r=["stg"], dma=True)
            A("act", lambda e: e.copy(out=mixT[:, :, 0:NS], in_=mixS[:]), r=["mixS"], w=["mixT"])
            ck(9)

            fpast = sb("fpast", [128, 16, NS, 2]); uraw = sb("uraw", [128, 16, NS])
            fst = stg

            def load_fpast(c0):
                cn = min(16, FC2 - c0)
                A("sp", lambda e: e.dma_start(out=fst[0:NS * 2, 0:cn * 128], in_=sffn_in.rearrange("b k c -> (b k) c")[:, c0 * 128:(c0 + cn) * 128]),
                  w=["stg"], dma=True)
                for i0 in range(0, cn, 4):
                    b = bank()
                    n4 = min(4, cn - i0)
                    for i in range(n4):
                        A("pe", lambda e, i0=i0, i=i, b=b: e.transpose(out=ps[b][:, i * 128:i * 128 + NS * 2], in_=fst[0:NS * 2, (i0 + i) * 128:(i0 + i + 1) * 128],
                                                                      identity=ident_f[0:NS * 2, 0:NS * 2]), r=["stg", "cst"], w=["ps%d" % b])
                    A("act", lambda e, i0=i0, n4=n4, b=b: e.copy(out=fpast[:, i0:i0 + n4, :, :].rearrange("p c b k -> p c (b k)"),
                                                                 in_=ps[b][:, :].rearrange("p (c x) -> p c x", c=4)[:, 0:n4, 0:NS * 2]), r=["ps%d" % b], w=["fpast"])

            def cross_s():
                for s in range(NS):
                    for (dst, dkey, srcap) in ((cmk_bf, "cmk_bf", cmk_in), (smv, "smv", cmv_in)):
                        for mb in range(MB):
                            i = wstate["i"]; wstate["i"] = (i + 1) % 2
                            A("sp", lambda e, s=s, mb=mb, i=i, srcap=srcap: e.dma_start(out=wstg[i][:, 0:512], in_=srcap[s, mb * 128:(mb + 1) * 128, :]), w=["wstg%d" % i], dma=True)
                            A("pool", lambda e, mb=mb, i=i, dst=dst: e.tensor_copy(out=dst[:, mb, :], in_=wstg[i][:, 0:512]), r=["wstg%d" % i], w=[dkey])
                    for mb in range(MB):
                        b = bank()
                        for h in range(4):
                            A("pe", lambda e, h=h, mb=mb, b=b: e.transpose(out=psbf(b)[:, h * 128:(h + 1) * 128], in_=cmk_bf[:, mb, h * 128:(h + 1) * 128], identity=ident_bf[:, :]),
                              r=["cmk_bf", "ident_bf"], w=["ps%d" % b])
                        A("act", lambda e, mb=mb, b=b: e.copy(out=smkT[:, :, mb * 128:(mb + 1) * 128], in_=psbf(b)[:, 0:512].rearrange("p (h m) -> p h m", h=4)),
                          r=["ps%d" % b], w=["smkT"])
                    cross_attn_one(s)

            def cross_attn_one(s):
                for h in range(4):
                    for mb in range(MB):
                        b = bank()
                        A("pe", lambda e, h=h, mb=mb, b=b: e.matmul(ps[b][:, 0:1], lhsT=smkT[:, h, mb * 128:(mb + 1) * 128], rhs=qcn[:, h, s:s + 1], start=True, stop=True),
                          r=["smkT", "qcn"], w=["ps%d" % b])
                        A("act", lambda e, mb=mb, b=b: e.activation(out=pc[mb % 2][:, 0:1], in_=ps[b][:, 0:1], func=AF.Exp, scale=MEM_D ** -0.5),
                          r=["ps%d" % b], w=["pc%d" % (mb % 2)])
                    bo, bd = bank(), bank()
                    for mb in range(MB):
                        A("pe", lambda e, h=h, mb=mb, bo=bo: e.matmul(ps[bo][:, 0:1], lhsT=smv[:, mb, h * 128:(h + 1) * 128], rhs=pc[mb % 2][:, 0:1],
                                                                     start=(mb == 0), stop=(mb == MB - 1)), r=["smv", "pc%d" % (mb % 2)], w=["ps%d" % bo])
                        A("pe", lambda e, mb=mb, bd=bd: e.matmul(ps[bd][:, 0:1], lhsT=ones_bf[:, :], rhs=pc[mb % 2][:, 0:1],
                                                                start=(mb == 0), stop=(mb == MB - 1)), r=["ones_bf", "pc%d" % (mb % 2)], w=["ps%d" % bd])
                    A("dve", lambda e, bd=bd: e.reciprocal(out=nrm_r[:, 0:1], in_=ps[bd][:, 0:1]), r=["ps%d" % bd], w=["nrm_r"])
                    A("dve", lambda e, h=h, bo=bo: e.tensor_tensor(out=ocT[:, h, s:s + 1], in0=ps[bo][:, 0:1], in1=nrm_r[:, 0:1], op=ALU.mult),
                      r=["ps%d" % bo, "nrm_r"], w=["ocT"])

            def ffn_conv_s(ch, b):
                wo = cfg.po["ffn_conv_w"][0] + ch * 3
                cl = ch % 16
                if cl == 0:
                    load_fpast(ch)
                A("act", lambda e: e.copy(out=uraw[:, cl, :], in_=ps[b][:, 0:NS]), r=["ps%d" % b], w=["uraw"])
                A("dve", lambda e: e.tensor_scalar(out=cacc[:, 0:NS], in0=ps[b][:, 0:NS], scalar1=prm[:, wo + 2:wo + 3], scalar2=pcol("ffn_conv_b", ch),
                                                   op0=ALU.mult, op1=ALU.add), r=["ps%d" % b, "prm"], w=["cacc"])
                for k in range(2):
                    A("dve", lambda e, k=k: e.scalar_tensor_tensor(out=cacc[:, 0:NS], in0=fpast[:, cl, :, k], scalar=prm[:, wo + k:wo + k + 1], in1=cacc[:, 0:NS],
                                                                   op0=ALU.mult, op1=ALU.add), r=["fpast", "cacc", "prm"], w=["cacc"])
                ffn_post(ch, NS)
                if cl == 15 or ch == FC2 - 1:
                    base = ch - cl
                    rows_out(lambda chh: uraw[:, chh, :], "uraw", cl + 1, NS, lambda c0, cn, base=base: o_fs[:, 1, (base + c0) * 128:(base + c0 + cn) * 128])
            stage_rest(NS, [(0, NS)], lambda tid: xt[0], lambda tid: "xt0", lambda tid: 0, cross_s, ffn_conv_s)
            A("sp", lambda e: e.dma_start(out=ys[:, :], in_=xt[0][0:NS, :]), r=["xt0"], dma=True)


        except _Stop:
            pass

        with nc.Block() as block:
            fns = {"pe": block.tensor, "act": block.scalar, "dve": block.vector, "pool": block.gpsimd, "sp": block.sync}
            P.emit(fns)
    return nc


def make_consts():
    p = np.arange(128)[:, None]; c = np.arange(128)[None, :]
    ident = (p == c); mcur = (p <= c); mprev = (p >= c); U = (p > c); bones = (p // 64 == c // 64)
    return np.concatenate([ident, mcur, mprev, U, bones], axis=1).astype(np.float32)


def make_params(cfg, inp):
    def fm(v):
        return np.ascontiguousarray(np.asarray(v, np.float32).reshape(-1, 128).T)
    cols = {}
    for nme in ("g_mix", "g_cross", "g_ffn", "g_mem", "ssd_norm_g"):
        cols[nme] = fm(inp[nme][0])
    cw = np.asarray(inp["ssd_conv_w"][0], np.float32)
    cols["ssd_conv_w"] = np.ascontiguousarray(cw.reshape(4, 24, 128).transpose(2, 1, 0).reshape(128, 96))
    cols["ssd_conv_b"] = fm(inp["ssd_conv_b"][0])
    fw = np.asarray(inp["ffn_conv_w"][0], np.float32)
    cols["ffn_conv_w"] = np.ascontiguousarray(fw.reshape(3, cfg.FC2, 128).transpose(2, 1, 0).reshape(128, 3 * cfg.FC2))
    cols["ffn_conv_b"] = fm(inp["ffn_conv_b"][0])
    cols["q_norm_g"] = np.tile(np.asarray(inp["q_norm_g"][0], np.float32), 2)[:, None]
    cols["k_norm_g"] = np.tile(np.asarray(inp["k_norm_g"][0], np.float32), 2)[:, None]
    cols["cq_norm_g"] = np.asarray(inp["cq_norm_g"][0], np.float32)[:, None]
    cols["ck_norm_g"] = np.asarray(inp["ck_norm_g"][0], np.float32)[:, None]
    cols["sinks"] = np.tile(np.asarray(inp["sinks"][0], np.float32)[None, :], (128, 1))
    z = np.zeros((128, 1), np.float32); z[:32, 0] = inp["dt_bias"][0]; cols["dt_bias"] = z
    z = np.zeros((128, 1), np.float32); z[:32, 0] = inp["a_log"][0]; cols["a_log"] = z
    cols["d_skip"] = fm(np.repeat(np.asarray(inp["d_skip"][0], np.float32), 64))
    out = np.zeros((128, cfg.PW), np.float32)
    for nme, (o, w) in cfg.po.items():
        assert cols[nme].shape == (128, w), (nme, cols[nme].shape, w)
        out[:, o:o + w] = cols[nme]
    return out


def make_in_maps(cfg, inp, n_cores, n_batch):
    consts = make_consts(); params = make_params(cfg, inp)
    NS = cfg.NS
    maps = []
    f = lambda a: np.ascontiguousarray(np.asarray(a, np.float32))
    for c in range(n_cores):
        b = c % n_batch
        s0 = c * NS
        m = {
            "x_prompt": f(inp["x_prompt"][b]), "x_sample": f(inp["x_sample"][s0:s0 + NS, 0]),
            "cache_swa_k": f(inp["cache_swa_k"][0, s0:s0 + NS]).reshape(NS, 128, D_KV),
            "cache_swa_v": f(inp["cache_swa_v"][0, s0:s0 + NS]).reshape(NS, 128, D_KV),
            "state_ssd_conv": f(inp["state_ssd_conv"][0, s0:s0 + NS]),
            "state_ssd": f(inp["state_ssd"][0, s0:s0 + NS]).reshape(NS, D_SSD, DST),
            "cache_mem_k": f(inp["cache_mem_k"][0, s0:s0 + NS]).reshape(NS, cfg.NMEM, D_X),
            "cache_mem_v": f(inp["cache_mem_v"][0, s0:s0 + NS]).reshape(NS, cfg.NMEM, D_X),
            "state_ffn_conv": f(inp["state_ffn_conv"][0, s0:s0 + NS]), "mem_prompt": f(inp["mem_prompt"][b]),
            "w_in": f(inp["w_in"][0]), "w_out": f(inp["w_out"][0]), "w_cq": f(inp["w_cq"][0]), "w_ck": f(inp["w_ck"][0]),
            "w_cv": f(inp["w_cv"][0]), "w_co": f(inp["w_co"][0]), "w_up": f(inp["w_up"][0]), "w_down": f(inp["w_down"][0]),
            "consts": consts, "params": params,
        }
        maps.append(m)
    return maps


def assemble(cfg, res, n_cores, n_batch):
    NS = cfg.NS
    B = n_batch
    g = lambda name, c: np.asarray(res[c][name], np.float32)
    cat_s = lambda name, shp: np.concatenate([g(name, c).reshape((NS,) + shp) for c in range(n_cores)], axis=0)
    st_p = lambda name, shp: np.stack([g(name, c).reshape(shp) for c in range(B)], axis=0)
    return (
        st_p("y_prompt", (cfg.SEQ, cfg.D)), cat_s("y_sample", (1, cfg.D)),
        st_p("swa_k_prompt", (128, N_KV, HD))[None], st_p("swa_v_prompt", (128, N_KV, HD))[None],
        cat_s("swa_k_sample", (128, N_KV, HD))[None], cat_s("swa_v_sample", (128, N_KV, HD))[None],
        st_p("ssd_conv_prompt", (3, CONV_DIM))[None], cat_s("ssd_conv_sample", (3, CONV_DIM))[None],
        st_p("ssd_state_prompt", (SSD_H, SSD_P, DST))[None], cat_s("ssd_state_sample", (SSD_H, SSD_P, DST))[None],
        st_p("mem_k_prompt", (cfg.NMEM, MEM_H, MEM_D))[None], st_p("mem_v_prompt", (cfg.NMEM, MEM_H, MEM_D))[None],
        st_p("ffn_conv_prompt", (2, 2 * cfg.DFF))[None], cat_s("ffn_conv_sample", (2, 2 * cfg.DFF))[None],
    )


def kernel(**inputs):
    cfg = Cfg()
    nc = build(cfg)
    maps = make_in_maps(cfg, inputs, 8, 4)
    res = run_bass_kernel_spmd(nc, maps, core_ids=list(range(8)))
    return assemble(cfg, res.results, 8, 4)
```

```python
import numpy as np
import concourse.bass as bass
import concourse.mybir as mybir
from concourse.bass_utils import run_bass_kernel_spmd
from contextlib import ExitStack

F32 = mybir.dt.float32
BF16 = mybir.dt.bfloat16
AF = mybir.ActivationFunctionType
ALU = mybir.AluOpType
AX = mybir.AxisListType

EPS = 1e-6
N_HEADS, N_KV, HD = 16, 4, 64
SSD_H, SSD_P, SSD_G, DST = 32, 64, 4, 128
D_ATTN, D_KV, D_SSD = 1024, 256, 2048
CONV_DIM = D_SSD + 2 * SSD_G * DST
D_IN = D_ATTN + 2 * D_KV + D_SSD + CONV_DIM + SSD_H
D_MIX = D_ATTN + D_SSD
MEM_H, MEM_D = 4, 128
D_X = 512
C_Q, C_K, C_V, C_Z, C_XBC, C_DT = 0, 1024, 1280, 1536, 3584, 6656


class _Stop(Exception):
    pass


class Cfg:
    def __init__(self, D=2048, SEQ=2048, NS=16, DFF=5632, NMEM=256, NTG=2, stop=None):
        self.stop = stop
        self.D, self.SEQ, self.NS, self.DFF, self.NMEM, self.NTG = D, SEQ, NS, DFF, NMEM, NTG
        self.KC = D // 128
        self.FC = DFF // 128
        self.FC2 = 2 * self.FC
        self.NT = SEQ // 128
        self.MB = NMEM // 128
        o = 0
        self.po = {}
        for name, w in [("g_mix", self.KC), ("g_cross", self.KC), ("g_ffn", self.KC), ("g_mem", self.KC),
                        ("ssd_norm_g", 16), ("ssd_conv_w", 96), ("ssd_conv_b", 24),
                        ("ffn_conv_w", 3 * self.FC2), ("ffn_conv_b", self.FC2),
                        ("q_norm_g", 1), ("k_norm_g", 1), ("cq_norm_g", 1), ("ck_norm_g", 1),
                        ("sinks", 16), ("dt_bias", 1), ("a_log", 1), ("d_skip", 16)]:
            self.po[name] = (o, w)
            o += w
        self.PW = o


ENGS = ("pe", "act", "dve", "pool", "sp")


class Prog:
    def __init__(self, nc, es):
        self.nc = nc
        self.ops = {e: [] for e in ENGS}
        self.lastw = {}
        self.rd_c = {}
        self.rd_d = {}
        self.esem = {e: es.enter_context(nc.semaphore("sem_" + e)) for e in ENGS if e != "sp"}
        self.ndsem = 40
        self.dsems = [es.enter_context(nc.semaphore("dsem%d" % i)) for i in range(self.ndsem)]
        self.dsem_val = [0] * self.ndsem
        self.dsem_last = [None] * self.ndsem
        self.dnext = {"sp": 0, "pool": 0, "act": 0}
        self.drange = {"sp": (0, 26), "pool": (26, 40), "act": (0, 26)}
        self.bank_i = 0

    ALIAS = {"zs": "R", "xc": "R", "qn": "R", "gT": "R"}

    def op(self, eng, fn, r=(), w=(), dma=False):
        r = [self.ALIAS.get(k, k) for k in r]
        w = [self.ALIAS.get(k, k) for k in w]
        if eng != "pe":
            w = w + [k for k in r if k.startswith("ps") and k[2:].isdigit() and k not in w]
        deps = set()
        for k in r:
            lw = self.lastw.get(k)
            if lw is not None:
                deps.add(lw)
        for k in w:
            lw = self.lastw.get(k)
            if lw is not None:
                deps.add(lw)
            for e2, i2 in self.rd_c.get(k, {}).items():
                deps.add((e2, i2))
            for idn in self.rd_d.get(k, ()):
                deps.add(idn)
        o = dict(eng=eng, fn=fn, deps=deps, dma=dma, idx=len(self.ops[eng]), signal=False)
        ident = (eng, o["idx"])
        if dma:
            lo, hi = self.drange[eng]
            s = lo + self.dnext[eng]
            self.dnext[eng] = (self.dnext[eng] + 1) % (hi - lo)
            if self.dsem_last[s] is not None:
                deps.add(self.dsem_last[s])
            self.dsem_val[s] += 16
            o["dsem"] = (s, self.dsem_val[s])
            self.dsem_last[s] = ident
        deps.discard(ident)
        self.ops[eng].append(o)
        for k in w:
            self.lastw[k] = ident
            self.rd_c[k] = {}
            self.rd_d[k] = []
        for k in r:
            if dma:
                self.rd_d.setdefault(k, []).append(ident)
            else:
                self.rd_c.setdefault(k, {})[eng] = o["idx"]
        return ident

    def emit(self, block_fns):
        for e in ENGS:
            for o in self.ops[e]:
                for (se, si) in o["deps"]:
                    so = self.ops[se][si]
                    if not so["dma"]:
                        if se == "pe" and e == "pe" and not o["dma"]:
                            continue
                        so["signal"] = True
        for e in ENGS:
            c = 0
            for o in self.ops[e]:
                if o["signal"]:
                    c += 1
                    o["sigval"] = c
        ops, esem, dsems = self.ops, self.esem, self.dsems

        def run(ename, eng):
            waited = {}
            for o in ops[ename]:
                for (se, si) in sorted(o["deps"]):
                    so = ops[se][si]
                    if so["dma"]:
                        s, v = so["dsem"]
                        key = ("d", s)
                        sem = dsems[s]
                    else:
                        if se == "pe" and ename == "pe" and not o["dma"]:
                            continue
                        key = ("e", se)
                        sem = esem[se]
                        v = so["sigval"]
                    if waited.get(key, 0) >= v:
                        continue
                    waited[key] = v
                    eng.wait_ge(sem, v)
                ins = o["fn"](eng)
                if o["dma"]:
                    ins.then_inc(dsems[o["dsem"][0]], 16)
                elif o["signal"]:
                    ins.then_inc(esem[ename], 1)
            final = {}
            for o in ops[ename]:
                if o["dma"]:
                    s, v = o["dsem"]
                    final[s] = max(final.get(s, 0), v)
            for s, v in sorted(final.items()):
                eng.wait_ge(dsems[s], v)

        for ename in ENGS:
            block_fns[ename](lambda eng, ename=ename: run(ename, eng))


def build(cfg):
    nc = bass.Bass("TRN2", target_bir_lowering=False)
    D, SEQ, NS, DFF, NMEM, NTG = cfg.D, cfg.SEQ, cfg.NS, cfg.DFF, cfg.NMEM, cfg.NTG
    KC, FC, FC2, NT, MB, PW = cfg.KC, cfg.FC, cfg.FC2, cfg.NT, cfg.MB, cfg.PW

    def din(name, shape):
        return nc.dram_tensor(name, list(shape), F32, kind="ExternalInput").ap()

    def dout(name, shape):
        return nc.dram_tensor(name, list(shape), F32, kind="ExternalOutput").ap()

    xp = din("x_prompt", [SEQ, D]); xs_in = din("x_sample", [NS, D])
    ck_in = din("cache_swa_k", [NS, 128, D_KV]); cv_in = din("cache_swa_v", [NS, 128, D_KV])
    sconv_in = din("state_ssd_conv", [NS, 3, CONV_DIM]); sst_in = din("state_ssd", [NS, D_SSD, DST])
    cmk_in = din("cache_mem_k", [NS, NMEM, D_X]); cmv_in = din("cache_mem_v", [NS, NMEM, D_X])
    sffn_in = din("state_ffn_conv", [NS, 2, 2 * DFF]); memp = din("mem_prompt", [NMEM, D])
    w_in = din("w_in", [D, D_IN]); w_out = din("w_out", [D_MIX, D]); w_cq = din("w_cq", [D, D_X])
    w_ck = din("w_ck", [D, D_X]); w_cv = din("w_cv", [D, D_X]); w_co = din("w_co", [D_X, D])
    w_up = din("w_up", [D, 2 * DFF]); w_down = din("w_down", [DFF, D])
    consts_in = din("consts", [128, 640]); params_in = din("params", [128, PW])

    yp = dout("y_prompt", [SEQ, D]); ys = dout("y_sample", [NS, D])
    o_kp = dout("swa_k_prompt", [128, D_KV]); o_vp = dout("swa_v_prompt", [128, D_KV])
    o_ks = dout("swa_k_sample", [NS, 128, D_KV]); o_vs = dout("swa_v_sample", [NS, 128, D_KV])
    o_cp = dout("ssd_conv_prompt", [3, CONV_DIM]); o_cs = dout("ssd_conv_sample", [NS, 3, CONV_DIM])
    o_sp = dout("ssd_state_prompt", [D_SSD, DST]); o_ss = dout("ssd_state_sample", [NS, D_SSD, DST])
    o_mk = dout("mem_k_prompt", [NMEM, D_X]); o_mv = dout("mem_v_prompt", [NMEM, D_X])
    o_fp = dout("ffn_conv_prompt", [2, 2 * DFF]); o_fs = dout("ffn_conv_sample", [NS, 2, 2 * DFF])

    es = ExitStack()
    with es:
        def sb(name, shape, dt=F32):
            return es.enter_context(nc.sbuf_tensor(name, list(shape), dt))

        P = Prog(nc, es)
        NG = NTG * 128
        ps = [es.enter_context(nc.psum_tensor("ps%d" % i, [128, 512], F32)) for i in range(8)]

        def bank():
            i = P.bank_i
            P.bank_i = (i + 1) % 8
            return i

        cst = sb("cst", [128, 640]); prm = sb("prm", [128, PW])
        ident_bf = sb("ident_bf", [128, 128], BF16)
        mcur4 = sb("mcur4", [128, 4, 128], BF16); mprev4 = sb("mprev4", [128, 4, 128], BF16)
        bones_bf = sb("bones_bf", [128, 128], BF16); ones_bf = sb("ones_bf", [128, 128], BF16)
        ones_f = sb("ones_f", [128, 128]); epsb = sb("epsb", [128, 1]); oneb = sb("oneb", [128, 1])
        esink = sb("esink", [128, 16]); a_neg = sb("a_neg", [128, 1])
        ident_f = cst[:, 0:128]; tri_f = cst[:, 128:256]; U_f = cst[:, 384:512]

        def pcol(name, j=0, n=1):
            o, w = cfg.po[name]
            return prm[:, o + j:o + j + n]

        WB = 2048
        NWB = 6
        wbufs = [sb("wbuf%d" % i, [128, WB], BF16) for i in range(NWB)]
        wstate = {"i": 0}

        A = P.op

        def ck(n):
            if cfg.stop == n:
                raise _Stop()

        try:
            A("sp", lambda e: e.dma_start(out=cst[:], in_=consts_in[:, :]), w=["cst"], dma=True)
            A("sp", lambda e: e.dma_start(out=prm[:], in_=params_in[:, :]), w=["prm"], dma=True)
            A("dve", lambda e: e.tensor_copy(out=ident_bf[:], in_=ident_f), r=["cst"], w=["ident_bf"])
            for g in range(4):
                A("dve", lambda e, g=g: e.tensor_copy(out=mcur4[:, g, :], in_=cst[:, 128:256]), r=["cst"], w=["mcur4"])
                A("dve", lambda e, g=g: e.tensor_copy(out=mprev4[:, g, :], in_=cst[:, 256:384]), r=["cst"], w=["mprev4"])
            A("dve", lambda e: e.tensor_copy(out=bones_bf[:], in_=cst[:, 512:640]), r=["cst"], w=["bones_bf"])
            A("dve", lambda e: e.memset(ones_bf[:], 1.0), w=["ones_bf"])
            A("dve", lambda e: e.memset(ones_f[:], 1.0), w=["ones_f"])
            A("dve", lambda e: e.memset(epsb[:], EPS), w=["epsb"])
            A("dve", lambda e: e.memset(oneb[:], 1.0), w=["oneb"])
            A("act", lambda e: e.activation(out=esink[:], in_=pcol("sinks", 0, 16), func=AF.Exp), r=["prm"], w=["esink"])
            A("act", lambda e: e.activation(out=a_neg[:], in_=pcol("a_log"), func=AF.Exp), r=["prm"], w=["a_neg"])
            A("dve", lambda e: e.tensor_scalar(out=a_neg[:], in0=a_neg[:], scalar1=-1.0, scalar2=None, op0=ALU.mult),
              r=["a_neg"], w=["a_neg"])

            ck(1)
            def load_w(w_ap, k0, kn, c0, cn):
                i = wstate["i"]; wstate["i"] = (i + 1) % NWB
                assert kn * cn <= WB
                view = wbufs[i][:, 0:kn * cn].rearrange("p (k c) -> p k c", k=kn)
                src = w_ap[k0 * 128:(k0 + kn) * 128, c0:c0 + cn].rearrange("(k p) c -> p k c", p=128)
                A("pool", lambda e: e.dma_start(out=view, in_=src), w=["wbuf%d" % i], dma=True)
                return view, "wbuf%d" % i

            def psbf(b):
                return ps[b][:].bitcast(BF16)

            ss_t = sb("ss_t", [128, 1]); rt_t = sb("rt_t", [128, 1]); rstd_t = sb("rstd_t", [128, 1])
            xn_t = sb("xn_t", [128, max(D, 2048)], BF16)

            def norm_T(x_tile, xkey, nrows, gname, out_fn, okey):
                A("act", lambda e: e.activation(out=xn_t[0:nrows, 0:D], in_=x_tile, func=AF.Square, accum_out=ss_t[0:nrows, :]),
                  r=[xkey], w=["xn_t", "ss_t"])
                A("act", lambda e: e.activation(out=rt_t[0:nrows, :], in_=ss_t[0:nrows, :], func=AF.Sqrt, scale=1.0 / D,
                                                bias=epsb[0:nrows, :]), r=["ss_t", "epsb"], w=["rt_t"])
                A("dve", lambda e: e.reciprocal(out=rstd_t[0:nrows, :], in_=rt_t[0:nrows, :]), r=["rt_t"], w=["rstd_t"])
                A("dve", lambda e: e.tensor_scalar(out=xn_t[0:nrows, 0:D], in0=x_tile, scalar1=rstd_t[0:nrows, 0:1], scalar2=None,
                                                   op0=ALU.mult), r=[xkey, "rstd_t"], w=["xn_t"])
                for k0 in range(0, KC, 8):
                    kn = min(8, KC - k0)
                    b = bank()
                    for j in range(kn):
                        kc = k0 + j
                        A("pe", lambda e, kc=kc, j=j, b=b: e.transpose(out=psbf(b)[:, j * 128:j * 128 + nrows],
                                                                      in_=xn_t[0:nrows, kc * 128:(kc + 1) * 128],
                                                                      identity=ident_bf[0:nrows, 0:nrows]),
                          r=["xn_t", "ident_bf"], w=["ps%d" % b])
                    for j in range(kn):
                        kc = k0 + j
                        A("act", lambda e, kc=kc, j=j, b=b: e.activation(out=out_fn(kc), in_=psbf(b)[:, j * 128:j * 128 + nrows],
                                                                        func=AF.Identity, scale=pcol(gname, kc)),
                          r=["ps%d" % b, "prm"], w=[okey])

            def linear_fm(w_ap, KN, c0, ncols, rhs_fn, rkeys, N, evac):
                ks = max(1, WB // 512)
                for s0 in range(0, ncols, 512):
                    sw = min(512, ncols - s0)
                    chunks = [(o0, min(128, sw - o0)) for o0 in range(0, sw, 128)]
                    banks = [bank() for _ in chunks]
                    for k0 in range(0, KN, ks):
                        kn = min(ks, KN - k0)
                        view, wkey = load_w(w_ap, k0, kn, c0 + s0, sw)
                        for (o0, m), b in zip(chunks, banks):
                            for kk in range(kn):
                                kc = k0 + kk
                                A("pe", lambda e, kc=kc, kk=kk, o0=o0, m=m, b=b, view=view: e.matmul(
                                    ps[b][0:m, 0:N], lhsT=view[:, kk, o0:o0 + m], rhs=rhs_fn(kc), start=(kc == 0), stop=(kc == KN - 1)),
                                  r=[wkey] + rkeys, w=["ps%d" % b])
                    for (o0, m), b in zip(chunks, banks):
                        evac(c0 + s0 + o0, m, b)

            def linear_tm(w_ap, KN, ncols, cg, lhsT_fn, lkeys, tiles, evac):
                ks = max(1, WB // cg)
                for c0 in range(0, ncols, cg):
                    cn = min(cg, ncols - c0)
                    banks = {tid: bank() for (tid, _) in tiles}
                    for k0 in range(0, KN, ks):
                        kn = min(ks, KN - k0)
                        view, wkey = load_w(w_ap, k0, kn, c0, cn)
                        for (tid, ntok) in tiles:
                            b = banks[tid]
                            for kk in range(kn):
                                kc = k0 + kk
                                A("pe", lambda e, kc=kc, kk=kk, tid=tid, ntok=ntok, b=b, view=view, cn=cn: e.matmul(
                                    ps[b][0:ntok, 0:cn], lhsT=lhsT_fn(kc, tid), rhs=view[:, kk, 0:cn], start=(kc == 0), stop=(kc == KN - 1)),
                                  r=[wkey] + lkeys, w=["ps%d" % b])
                    for (tid, ntok) in tiles:
                        evac(tid, ntok, c0, cn, banks[tid])

            sq_t = sb("sq_t", [128, 256], BF16); nrm_r = sb("nrm_r", [128, 256]); nrm_s = sb("nrm_s", [128, 256])

            def headnorm_fm(src_fn, skey, nchunks, N, ones_ap, okey_ones, dim, gname, out_bf_fn, obkey, out_f_fn=None, ofkey=None):
                for c in range(nchunks):
                    b = bank()
                    A("act", lambda e, c=c: e.activation(out=sq_t[:, 0:N], in_=src_fn(c), func=AF.Square), r=[skey], w=["sq_t"])
                    A("pe", lambda e, b=b: e.matmul(ps[b][:, 0:N], lhsT=ones_ap, rhs=sq_t[:, 0:N], start=True, stop=True),
                      r=["sq_t", okey_ones], w=["ps%d" % b])
                    A("act", lambda e, b=b: e.activation(out=nrm_s[:, 0:N], in_=ps[b][:, 0:N], func=AF.Sqrt, scale=1.0 / dim, bias=epsb[:]),
                      r=["ps%d" % b, "epsb"], w=["nrm_s"])
                    A("dve", lambda e: e.reciprocal(out=nrm_r[:, 0:N], in_=nrm_s[:, 0:N]), r=["nrm_s"], w=["nrm_r"])
                    A("dve", lambda e, c=c: e.scalar_tensor_tensor(out=out_bf_fn(c), in0=src_fn(c), scalar=pcol(gname), in1=nrm_r[:, 0:N],
                                                                   op0=ALU.mult, op1=ALU.mult), r=[skey, "nrm_r", "prm"], w=[obkey])
                    if out_f_fn is not None:
                        A("dve", lambda e, c=c: e.scalar_tensor_tensor(out=out_f_fn(c), in0=src_fn(c), scalar=pcol(gname), in1=nrm_r[:, 0:N],
                                                                       op0=ALU.mult, op1=ALU.mult), r=[skey, "nrm_r", "prm"], w=[ofkey])

            xt = [sb("xt%d" % i, [128, D]) for i in range(NTG)]
            stg = sb("stg", [128, 2048])
            hT = sb("hT", [128, KC, NG], BF16)
            qf = sb("qf", [128, 8, NG], BF16)
            RR = sb("RR", [128, 48 * NG], BF16)
            qn = RR[:, 40 * NG:48 * NG].rearrange("p (c n) -> p c n", c=8)
            kf = sb("kf", [128, 4, NG], BF16); kn_bf = sb("kn_bf", [128, 4, NG], BF16); knf = sb("knf", [128, 4, NG])
            vTf = sb("vTf", [128, 2, NG])
            zs = RR[:, 0:16 * NG].rearrange("p (c n) -> p c n", c=16)
            xc = RR[:, 16 * NG:40 * NG].rearrange("p (c n) -> p c n", c=24)
            dtT = sb("dtT", [32, NG]); dtaT = sb("dtaT", [32, NG])
            mixT = sb("mixT", [128, 24, NG], BF16)
            ctmp = sb("ctmp", [128, 3 + NG]); cacc = sb("cacc", [128, NG])
            convc = sb("convc", [128, 24, 3]); uc = sb("uc", [128, FC2, 2])
            kd_prev = sb("kd_prev", [128, 4, 128], BF16); vd_prev = sb("vd_prev", [128, 4, 2, 64], BF16)
            vd_cur = sb("vd_cur", [128, 4, 2, 64], BF16)
            hst_f = sb("hst_f", [128, D_SSD]); hst_bf = sb("hst_bf", [128, D_SSD], BF16)
            memkT = sb("memkT", [128, 4, NMEM], BF16); memv = sb("memv", [128, MB, D_X], BF16)
            assert FC <= 48
            gT = RR[:, 0:FC * NG].rearrange("p (c n) -> p c n", c=FC)
            qcf = sb("qcf", [128, 4, NG], BF16); qcn = sb("qcn", [128, 4, NG], BF16); ocT = sb("ocT", [128, 4, NG], BF16)

            pT = [sb("pT%d" % i, [128, 512], BF16) for i in range(2)]
            rden = sb("rden", [128, 512])

            def attn_tile(q_fn, qkey, nq, kcur_fn, kckey, vcur, vckey, ncur, kprev_fn, kpkey, vprev, vpkey, nprev, out_fn, okey, masks):
                for kvh in range(4):
                    blocks = []
                    if nprev:
                        blocks.append((kprev_fn, kpkey, vprev, vpkey, nprev, mprev4, "mprev4"))
                    blocks.append((kcur_fn, kckey, vcur, vckey, ncur, mcur4, "mcur4"))
                    pts = []
                    for bi, (kfn, kkey, vv, vkey, ns, msk, mkey) in enumerate(blocks):
                        bAB = (bank(), bank())
                        for g in range(4):
                            h = kvh * 4 + g
                            c, half = h // 2, h % 2
                            gg = g // 2
                            b = bAB[half]
                            A("pe", lambda e, gg=gg, c=c, half=half, b=b, kfn=kfn, ns=ns, kvh=kvh: e.matmul(
                                ps[b][0:ns, gg * 128:gg * 128 + nq], lhsT=kfn(kvh)[half * 64:(half + 1) * 64, 0:ns],
                                rhs=q_fn(c)[half * 64:(half + 1) * 64, 0:nq], start=True, stop=True),
                              r=[kkey, qkey], w=["ps%d" % b])
                        pt = pT[bi]
                        ptv = pt[0:ns, :].rearrange("p (g q) -> p g q", g=4)[:, :, 0:nq]
                        for half in range(2):
                            b = bAB[half]
                            psv = ps[b][0:ns, 0:256].rearrange("p (g q) -> p g q", g=2)[:, :, 0:nq]
                            A("act", lambda e, ptv=ptv, psv=psv, half=half: e.activation(out=ptv[:, half * 2:half * 2 + 2, :], in_=psv, func=AF.Exp, scale=HD ** -0.5),
                              r=["ps%d" % b], w=["pT%d" % bi])
                        if masks:
                            A("dve", lambda e, ptv=ptv, msk=msk, ns=ns: e.tensor_tensor(out=ptv, in0=ptv, in1=msk[0:ns, :, 0:nq], op=ALU.mult),
                              r=["pT%d" % bi, mkey], w=["pT%d" % bi])
                        pts.append((ptv, "pT%d" % bi, vv, vkey, ns))
                    ck(41)
                    bo, bd = bank(), bank()
                    for bi, (ptv, pkey, vv, vkey, ns) in enumerate(pts):
                        A("pe", lambda e, ptv=ptv, vv=vv, ns=ns, bi=bi, bo=bo, kvh=kvh, npts=len(pts): e.matmul(
                            ps[bo][:, 0:4 * nq].rearrange("p (g q) -> p g q", g=4), lhsT=vv[0:ns, kvh, :, :].rearrange("p a d -> p (a d)"),
                            rhs=ptv, start=(bi == 0), stop=(bi == npts - 1)), r=[pkey, vkey], w=["ps%d" % bo])
                        A("pe", lambda e, ptv=ptv, ns=ns, bi=bi, bd=bd, npts=len(pts): e.matmul(
                            ps[bd][:, 0:4 * nq].rearrange("p (g q) -> p g q", g=4), lhsT=ones_bf[0:ns, :],
                            rhs=ptv, start=(bi == 0), stop=(bi == npts - 1)), r=[pkey, "ones_bf"], w=["ps%d" % bd])
                    ck(42)
                    for g in range(4):
                        h = kvh * 4 + g
                        gi = (g % 2) * 2 + g // 2
                        A("dve", lambda e, gi=gi, h=h, bd=bd: e.tensor_scalar(out=rden[:, gi * nq:(gi + 1) * nq], in0=ps[bd][:, gi * nq:(gi + 1) * nq],
                                                                              scalar1=esink[:, h:h + 1], scalar2=None, op0=ALU.add),
                          r=["ps%d" % bd, "esink"], w=["rden"])
                    A("dve", lambda e: e.reciprocal(out=rden[:, 0:4 * nq], in_=rden[:, 0:4 * nq]), r=["rden"], w=["rden"])
                    for g in range(4):
                        h = kvh * 4 + g
                        c, half = h // 2, h % 2
                        gi = (g % 2) * 2 + g // 2
                        A("dve", lambda e, gi=gi, c=c, half=half, bo=bo: e.tensor_tensor(
                            out=out_fn(c, half), in0=ps[bo][half * 64:(half + 1) * 64, gi * nq:(gi + 1) * nq],
                            in1=rden[half * 64:(half + 1) * 64, gi * nq:(gi + 1) * nq], op=ALU.mult),
                          r=["ps%d" % bo, "rden"], w=[okey])

            xdt = sb("xdt", [128, 32, 64], BF16); xdd = sb("xdd", [128, 32, 64], BF16)
            b_tok = sb("b_tok", [128, 4, 128], BF16)
            dt_tok = sb("dt_tok", [128, 32]); dta_tok = sb("dta_tok", [128, 32]); dta_exp = sb("dta_exp", [128, 4, 64])
            la_t = sb("la_t", [128, 32]); dec_end = sb("dec_end", [128, 32]); cd_bc = sb("cd_bc", [128, 32])
            cbm = sb("cbm", [128, 4, 128])
            dU = [sb("dU%d" % i, [128, 128]) for i in range(2)]
            eseg = sb("eseg", [128, 4, 128]); MT = [sb("MT%d" % i, [128, 4, 128], BF16) for i in range(2)]
            ela = eseg[:, 2:4, :]; ytmp = eseg[:, 0:2, :]
            ybuf = sb("ybuf", [128, 16, 128]); ysq = xn_t[:, 0:2048].rearrange("p (j l) -> p j l", j=16)

            def ssd_tile(L, xc_fn, xkey, dt_ap, dta_ap, dkey, zs_fn, zkey, out_fn, okey):
                xs_bks = []
                for j0 in range(0, 16, 8):
                    b = bank()
                    for j in range(8):
                        A("pe", lambda e, j=j, j0=j0, b=b: e.transpose(out=psbf(b)[0:L, j * 128:(j + 1) * 128], in_=xc_fn(j0 + j), identity=ident_bf[:, :]),
                          r=[xkey, "ident_bf"], w=["ps%d" % b])
                    xs_bks.append((j0, b))
                b = bank()
                for g in range(4):
                    A("pe", lambda e, g=g, b=b: e.transpose(out=psbf(b)[0:L, g * 128:(g + 1) * 128], in_=xc_fn(16 + g), identity=ident_bf[:, :]),
                      r=[xkey, "ident_bf"], w=["ps%d" % b])
                A("act", lambda e, b=b: e.copy(out=b_tok[0:L, :, :].rearrange("p g n -> p (g n)"), in_=psbf(b)[0:L, 0:512]), r=["ps%d" % b], w=["b_tok"])
                b = bank()
                A("pe", lambda e, b=b: e.transpose(out=ps[b][0:L, 0:32], in_=dt_ap, identity=ident_f[0:32, 0:32]), r=[dkey, "cst"], w=["ps%d" % b])
                A("pe", lambda e, b=b: e.transpose(out=ps[b][0:L, 32:64], in_=dta_ap, identity=ident_f[0:32, 0:32]), r=[dkey, "cst"], w=["ps%d" % b])
                A("act", lambda e, b=b: e.copy(out=dt_tok[0:L, :], in_=ps[b][0:L, 0:32]), r=["ps%d" % b], w=["dt_tok"])
                A("act", lambda e, b=b: e.copy(out=dta_tok[0:L, :], in_=ps[b][0:L, 32:64]), r=["ps%d" % b], w=["dta_tok"])
                ck(50)
                b = bank()
                A("pe", lambda e, b=b: e.matmul(ps[b][0:L, 0:32], lhsT=tri_f[0:L, 0:L], rhs=dta_tok[0:L, :], start=True, stop=True),
                  r=["cst", "dta_tok"], w=["ps%d" % b])
                A("pe", lambda e, b=b: e.matmul(ps[b][0:128, 32:64], lhsT=ones_f[0:L, :], rhs=dta_tok[0:L, :], start=True, stop=True),
                  r=["ones_f", "dta_tok"], w=["ps%d" % b])
                A("act", lambda e, b=b: e.copy(out=la_t[0:L, :], in_=ps[b][0:L, 0:32]), r=["ps%d" % b], w=["la_t"])
                A("dve", lambda e, b=b: e.tensor_tensor(out=dec_end[0:L, :], in0=ps[b][0:L, 32:64], in1=la_t[0:L, :], op=ALU.subtract),
                  r=["ps%d" % b, "la_t"], w=["dec_end"])
                A("act", lambda e: e.activation(out=dec_end[0:L, :], in_=dec_end[0:L, :], func=AF.Exp), r=["dec_end"], w=["dec_end"])
                A("act", lambda e, b=b: e.activation(out=cd_bc[:, :], in_=ps[b][:, 32:64], func=AF.Exp), r=["ps%d" % b], w=["cd_bc"])
                ck(51)
                for (j0, bx) in xs_bks:
                    A("dve", lambda e, j0=j0, bx=bx: e.tensor_tensor(out=xdt[0:L, j0 * 2:(j0 + 8) * 2, :],
                                                                     in0=psbf(bx)[0:L, 0:1024].rearrange("p (h d) -> p h d", h=16),
                                                                     in1=dt_tok[0:L, j0 * 2:(j0 + 8) * 2].unsqueeze(2).broadcast_to([L, 16, 64]), op=ALU.mult),
                      r=["ps%d" % bx, "dt_tok"], w=["xdt"])
                A("dve", lambda e: e.tensor_tensor(out=xdd[0:L], in0=xdt[0:L], in1=dec_end[0:L, :].unsqueeze(2).broadcast_to([L, 32, 64]), op=ALU.mult),
                  r=["xdt", "dec_end"], w=["xdd"])
                ck(52)
                b = bank()
                for g in range(4):
                    A("pe", lambda e, g=g, b=b: e.matmul(ps[b][0:L, g * 128:g * 128 + L], lhsT=xc_fn(16 + g), rhs=xc_fn(20 + g), start=True, stop=True),
                      r=[xkey], w=["ps%d" % b])
                A("dve", lambda e, b=b: e.tensor_tensor(out=cbm[0:L, :, 0:L], in0=ps[b][0:L, :].rearrange("p (g l) -> p g l", g=4)[:, :, 0:L],
                                                        in1=mcur4[0:L, :, 0:L], op=ALU.mult), r=["ps%d" % b, "mcur4"], w=["cbm"])
                ck(53)
                for q4 in range(8):
                    b = bank()
                    for i in range(4):
                        h = q4 * 4 + i
                        du = dU[h % 2]
                        A("dve", lambda e, h=h, du=du: e.tensor_scalar(out=du[0:L, 0:L], in0=U_f[0:L, 0:L], scalar1=dta_tok[0:L, h:h + 1], scalar2=None, op0=ALU.mult),
                          r=["cst", "dta_tok"], w=["dU%d" % (h % 2)])
                        A("pe", lambda e, i=i, du=du, b=b: e.matmul(ps[b][0:L, i * 128:i * 128 + L], lhsT=du[0:L, 0:L], rhs=tri_f[0:L, 0:L], start=True, stop=True),
                          r=["dU%d" % (h % 2), "cst"], w=["ps%d" % b])
                    ck(54)
                    A("act", lambda e, b=b: e.activation(out=eseg[0:L, :, 0:L], in_=ps[b][0:L, :].rearrange("p (g l) -> p g l", g=4)[:, :, 0:L], func=AF.Exp),
                      r=["ps%d" % b], w=["eseg"])
                    g = q4 // 2
                    mt = MT[q4 % 2]
                    A("dve", lambda e, g=g, mt=mt: e.tensor_tensor(out=mt[0:L, :, 0:L], in0=eseg[0:L, :, 0:L],
                                                                   in1=cbm[0:L, g:g + 1, 0:L].broadcast_to([L, 4, L]), op=ALU.mult),
                      r=["eseg", "cbm"], w=["MT%d" % (q4 % 2)])
                    ck(55)
                    A("dve", lambda e, q4=q4: e.tensor_copy(out=dta_exp[0:L], in_=dta_tok[0:L, q4 * 4:(q4 + 1) * 4].unsqueeze(2).broadcast_to([L, 4, 64])),
                      r=["dta_tok"], w=["dta_exp"])
                    by, bi_, be = bank(), bank(), bank()
                    for jj in range(2):
                        j = q4 * 2 + jj
                        for h2 in range(2):
                            i = jj * 2 + h2
                            A("pe", lambda e, j=j, h2=h2, i=i, jj=jj, mt=mt, by=by: e.matmul(
                                ps[by][h2 * 64:(h2 + 1) * 64, jj * 128:jj * 128 + L], lhsT=xdt[0:L, 2 * j + h2, :], rhs=mt[0:L, i, 0:L], start=True, stop=True),
                              r=["xdt", "MT%d" % (q4 % 2)], w=["ps%d" % by])
                        A("pe", lambda e, j=j, jj=jj, g=g, bi_=bi_: e.matmul(ps[bi_][:, jj * 128:jj * 128 + L], lhsT=hst_bf[:, j * 128:(j + 1) * 128], rhs=xc_fn(20 + g),
                                                                          start=True, stop=True), r=["hst_bf", xkey], w=["ps%d" % bi_])
                        A("pe", lambda e, j=j, jj=jj, be=be: e.matmul(ps[be][:, jj * 128:jj * 128 + L], lhsT=dta_exp[0:L, 2 * jj:2 * jj + 2, :].rearrange("p h d -> p (h d)"),
                                                                      rhs=tri_f[0:L, 0:L], start=True, stop=True), r=["dta_exp", "cst"], w=["ps%d" % be])
                    ck(56)
                    A("act", lambda e, be=be: e.activation(out=ela[:, 0:2, 0:L], in_=ps[be][:, 0:256].rearrange("p (g l) -> p g l", g=2)[:, :, 0:L], func=AF.Exp),
                      r=["ps%d" % be], w=["eseg"])
                    A("dve", lambda e, bi_=bi_: e.tensor_tensor(out=ytmp[:, 0:2, 0:L], in0=ps[bi_][:, 0:256].rearrange("p (g l) -> p g l", g=2)[:, :, 0:L],
                                                                in1=ela[:, 0:2, 0:L], op=ALU.mult), r=["ps%d" % bi_, "eseg"], w=["eseg"])
                    A("dve", lambda e, by=by: e.tensor_tensor(out=ytmp[:, 0:2, 0:L], in0=ps[by][:, 0:256].rearrange("p (g l) -> p g l", g=2)[:, :, 0:L],
                                                              in1=ytmp[:, 0:2, 0:L], op=ALU.add), r=["ps%d" % by, "eseg"], w=["eseg"])
                    for jj in range(2):
                        j = q4 * 2 + jj
                        A("dve", lambda e, j=j, jj=jj: e.scalar_tensor_tensor(out=ybuf[:, j, 0:L], in0=xc_fn(j), scalar=pcol("d_skip", j), in1=ytmp[:, jj, 0:L],
                                                                              op0=ALU.mult, op1=ALU.add), r=[xkey, "eseg", "prm"], w=["ybuf"])
                ck(57)
                for g in range(4):
                    b = bank()
                    A("pe", lambda e, g=g, b=b: e.matmul(ps[b][:, 0:512], lhsT=b_tok[0:L, g, :], rhs=xdd[0:L, g * 8:(g + 1) * 8, :].rearrange("p h d -> p (h d)"),
                                                         start=True, stop=True), r=["b_tok", "xdd"], w=["ps%d" % b])
                    ck(570)
                    hv = hst_f[:, g * 512:(g + 1) * 512].rearrange("p (h d) -> p h d", h=8)
                    A("dve", lambda e, g=g, hv=hv: e.tensor_tensor(out=hv, in0=hv, in1=cd_bc[:, g * 8:(g + 1) * 8].unsqueeze(2).broadcast_to([128, 8, 64]), op=ALU.mult),
                      r=["hst_f", "cd_bc"], w=["hst_f"])
                    ck(571)
                    A("dve", lambda e, g=g, b=b: e.tensor_tensor(out=hst_f[:, g * 512:(g + 1) * 512], in0=ps[b][:, 0:512], in1=hst_f[:, g * 512:(g + 1) * 512], op=ALU.add),
                      r=["ps%d" % b, "hst_f"], w=["hst_f"])
                    ck(572)
                ck(573)
                A("dve", lambda e: e.tensor_copy(out=hst_bf[:], in_=hst_f[:]), r=["hst_f", "hst_bf"], w=["hst_bf"])
                ck(58)
                ybv = ybuf[:, :, 0:L]
                A("dve", lambda e: e.tensor_tensor(out=ybv, in0=ybv, in1=zs_fn(), op=ALU.mult), r=["ybuf", zkey], w=["ybuf"])
                A("act", lambda e: e.activation(out=ysq[:, :, 0:L], in_=ybv, func=AF.Square), r=["ybuf"], w=["xn_t"])
                b = bank()
                for j in range(16):
                    A("pe", lambda e, j=j, b=b: e.matmul(ps[b][:, 0:L], lhsT=ones_bf[:, :], rhs=ysq[:, j, 0:L], start=(j == 0), stop=(j == 15)),
                      r=["xn_t", "ones_bf"], w=["ps%d" % b])
                A("act", lambda e, b=b: e.activation(out=nrm_s[:, 0:L], in_=ps[b][:, 0:L], func=AF.Sqrt, scale=1.0 / D_SSD, bias=epsb[:]),
                  r=["ps%d" % b, "epsb"], w=["nrm_s"])
                A("dve", lambda e: e.reciprocal(out=nrm_r[:, 0:L], in_=nrm_s[:, 0:L]), r=["nrm_s"], w=["nrm_r"])
                for j in range(16):
                    A("dve", lambda e, j=j: e.scalar_tensor_tensor(out=out_fn(j), in0=ybuf[:, j, 0:L], scalar=pcol("ssd_norm_g", j), in1=nrm_r[:, 0:L],
                                                                   op0=ALU.mult, op1=ALU.mult), r=["ybuf", "nrm_r", "prm"], w=[okey])

            pc = [sb("pc%d" % i, [128, 512], BF16) for i in range(2)]

            def cross_attn(N, kT, kkey, vv, vkey):
                for h in range(4):
                    pts = []
                    for mb in range(MB):
                        b = bank()
                        A("pe", lambda e, h=h, mb=mb, b=b: e.matmul(ps[b][:, 0:N], lhsT=kT[:, h, mb * 128:(mb + 1) * 128], rhs=qcn[:, h, 0:N], start=True, stop=True),
                          r=[kkey, "qcn"], w=["ps%d" % b])
                        A("act", lambda e, mb=mb, b=b: e.activation(out=pc[mb % 2][:, 0:N], in_=ps[b][:, 0:N], func=AF.Exp, scale=MEM_D ** -0.5),
                          r=["ps%d" % b], w=["pc%d" % (mb % 2)])
                        pts.append(mb)
                        if mb % 2 == 1 or mb == MB - 1:
                            pass
                    bo, bd = bank(), bank()
                    for mb in range(MB):
                        A("pe", lambda e, h=h, mb=mb, bo=bo: e.matmul(ps[bo][:, 0:N], lhsT=vv[:, mb, h * 128:(h + 1) * 128], rhs=pc[mb % 2][:, 0:N],
                                                                     start=(mb == 0), stop=(mb == MB - 1)), r=[vkey, "pc%d" % (mb % 2)], w=["ps%d" % bo])
                        A("pe", lambda e, mb=mb, bd=bd: e.matmul(ps[bd][:, 0:N], lhsT=ones_bf[:, :], rhs=pc[mb % 2][:, 0:N],
                                                                start=(mb == 0), stop=(mb == MB - 1)), r=["ones_bf", "pc%d" % (mb % 2)], w=["ps%d" % bd])
                    A("dve", lambda e, bd=bd: e.reciprocal(out=nrm_r[:, 0:N], in_=ps[bd][:, 0:N]), r=["ps%d" % bd], w=["nrm_r"])
                    A("dve", lambda e, h=h, bo=bo: e.tensor_tensor(out=ocT[:, h, 0:N], in0=ps[bo][:, 0:N], in1=nrm_r[:, 0:N], op=ALU.mult),
                      r=["ps%d" % bo, "nrm_r"], w=["ocT"])

            def conv_fm(b, m, N, taps, wname, bname, ch, carry_ap, ckey, out_ap, okey, func, out2=None, o2key=None):
                H = taps - 1
                A("act", lambda e: e.copy(out=ctmp[0:m, H:H + N], in_=ps[b][0:m, 0:N]), r=["ps%d" % b], w=["ctmp"])
                A("dve", lambda e: e.tensor_copy(out=ctmp[0:m, 0:H], in_=carry_ap), r=[ckey], w=["ctmp"])
                A("dve", lambda e: e.tensor_copy(out=carry_ap, in_=ctmp[0:m, N:N + H]), r=["ctmp"], w=[ckey])
                wo = cfg.po[wname][0] + ch * taps
                A("dve", lambda e: e.tensor_scalar(out=cacc[0:m, 0:N], in0=ctmp[0:m, H:H + N], scalar1=prm[0:m, wo + H:wo + H + 1],
                                                   scalar2=pcol(bname, ch)[0:m, :], op0=ALU.mult, op1=ALU.add), r=["ctmp", "prm"], w=["cacc"])
                for k in range(H):
                    A("dve", lambda e, k=k: e.scalar_tensor_tensor(out=cacc[0:m, 0:N], in0=ctmp[0:m, k:k + N], scalar=prm[0:m, wo + k:wo + k + 1],
                                                                   in1=cacc[0:m, 0:N], op0=ALU.mult, op1=ALU.add), r=["ctmp", "cacc", "prm"], w=["cacc"])
                if out_ap is None:
                    pass
                elif func is None:
                    A("act", lambda e: e.copy(out=out_ap, in_=cacc[0:m, 0:N]), r=["cacc"], w=[okey])
                else:
                    A("act", lambda e: e.activation(out=out_ap, in_=cacc[0:m, 0:N], func=func), r=["cacc"], w=[okey])
                if out2 is not None:
                    A("act", lambda e: e.activation(out=out2, in_=cacc[0:m, 0:N], func=func), r=["cacc"], w=[o2key])

            def rows_out(src_fn, skey, nch, n, dst_fn):
                for c0 in range(0, nch, 16):
                    cn = min(16, nch - c0)
                    for i0 in range(0, cn, 4):
                        b = bank()
                        n4 = min(4, cn - i0)
                        for i in range(n4):
                            A("pe", lambda e, c0=c0, i0=i0, i=i, b=b: e.transpose(out=ps[b][0:n, i * 128:(i + 1) * 128], in_=src_fn(c0 + i0 + i), identity=ident_f),
                              r=[skey, "cst"], w=["ps%d" % b])
                        A("act", lambda e, i0=i0, n4=n4, b=b: e.copy(out=stg[0:n, i0 * 128:(i0 + n4) * 128], in_=ps[b][0:n, 0:n4 * 128]), r=["ps%d" % b], w=["stg"])
                    A("sp", lambda e, c0=c0, cn=cn: e.dma_start(out=dst_fn(c0, cn), in_=stg[0:n, 0:cn * 128]), r=["stg"], dma=True)

            otok = sb("otok", [128, 512])
            for mb in range(MB):
                A("sp", lambda e, mb=mb: e.dma_start(out=xt[0][:], in_=memp[mb * 128:(mb + 1) * 128, :]), w=["xt0"], dma=True)
                norm_T(xt[0][:], "xt0", 128, "g_mem", lambda kc: hT[:, kc, 0:128], "hT")
                ck(11)

                def ev_mk(c, m, b):
                    A("act", lambda e: e.copy(out=qcf[:, c // 128, 0:128], in_=ps[b][:, 0:128]), r=["ps%d" % b], w=["qcf"])
                linear_fm(w_ck, KC, 0, D_X, lambda kc: hT[:, kc, 0:128], ["hT"], 128, ev_mk)
                ck(12)
                headnorm_fm(lambda c: qcf[:, c, 0:128], "qcf", 4, 128, ones_bf[:, :], "ones_bf", MEM_D, "ck_norm_g",
                            lambda c, mb=mb: memkT[:, c, mb * 128:(mb + 1) * 128], "memkT", lambda c: knf[:, c, 0:128], "knf")
                b = bank()
                for h in range(4):
                    A("pe", lambda e, h=h, b=b: e.transpose(out=ps[b][:, h * 128:(h + 1) * 128], in_=knf[:, h, 0:128], identity=ident_f),
                      r=["knf", "cst"], w=["ps%d" % b])
                A("act", lambda e, b=b: e.copy(out=otok[:], in_=ps[b][:, :]), r=["ps%d" % b], w=["otok"])
                A("sp", lambda e, mb=mb: e.dma_start(out=o_mk[mb * 128:(mb + 1) * 128, :], in_=otok[:]), r=["otok"], dma=True)
                ck(14)

                def ev_mv(tid, ntok, c0, cn, b, mb=mb):
                    A("act", lambda e: e.copy(out=otok[:, c0:c0 + cn], in_=ps[b][:, 0:cn]), r=["ps%d" % b], w=["otok"])
                    A("dve", lambda e: e.tensor_copy(out=memv[:, mb, c0:c0 + cn], in_=ps[b][:, 0:cn]), r=["ps%d" % b], w=["memv"])
                    A("sp", lambda e: e.dma_start(out=o_mv[mb * 128:(mb + 1) * 128, c0:c0 + cn], in_=otok[:, c0:c0 + cn]), r=["otok"], dma=True)
                linear_tm(w_cv, KC, D_X, 512, lambda kc, tid: hT[:, kc, 0:128], ["hT"], [(0, 128)], ev_mv)
                ck(15)

            ck(2)
            def stage_inproj(N, conv_mode):
                def ev(c, m, b):
                    if c < C_K:
                        A("act", lambda e: e.copy(out=qf[:, c // 128, 0:N], in_=ps[b][:, 0:N]), r=["ps%d" % b], w=["qf"])
                    elif c < C_V:
                        ch = (c - C_K) // 128
                        for half in range(2):
                            kvh = ch * 2 + half
                            for dst in range(2):
                                eng = "act" if dst == 0 else "dve"
                                if eng == "act":
                                    A("act", lambda e, kvh=kvh, half=half, dst=dst: e.copy(out=kf[dst * 64:(dst + 1) * 64, kvh, 0:N],
                                                                                          in_=ps[b][half * 64:(half + 1) * 64, 0:N]), r=["ps%d" % b], w=["kf"])
                                else:
                                    A("dve", lambda e, kvh=kvh, half=half, dst=dst: e.tensor_copy(out=kf[dst * 64:(dst + 1) * 64, kvh, 0:N],
                                                                                                 in_=ps[b][half * 64:(half + 1) * 64, 0:N]), r=["ps%d" % b], w=["kf"])
                    elif c < C_Z:
                        A("act", lambda e: e.copy(out=vTf[:, (c - C_V) // 128, 0:N], in_=ps[b][:, 0:N]), r=["ps%d" % b], w=["vTf"])
                    elif c < C_XBC:
                        A("act", lambda e: e.activation(out=zs[:, (c - C_Z) // 128, 0:N], in_=ps[b][:, 0:N], func=AF.Silu), r=["ps%d" % b], w=["zs"])
                    elif c < C_DT:
                        conv_mode((c - C_XBC) // 128, b)
                    else:
                        A("act", lambda e: e.activation(out=dtT[:, 0:N], in_=ps[b][0:32, 0:N], func=AF.Exp, bias=pcol("dt_bias")[0:32, :]),
                          r=["ps%d" % b, "prm"], w=["dtT"])
                        A("act", lambda e: e.activation(out=dtT[:, 0:N], in_=dtT[:, 0:N], func=AF.Ln, bias=oneb[0:32, :]), r=["dtT", "oneb"], w=["dtT"])
                        A("dve", lambda e: e.tensor_scalar(out=dtaT[:, 0:N], in0=dtT[:, 0:N], scalar1=a_neg[0:32, 0:1], scalar2=None, op0=ALU.mult),
                          r=["dtT", "a_neg"], w=["dtaT"])
                linear_fm(w_in, KC, 0, D_IN, lambda kc: hT[:, kc, 0:N], ["hT"], N, ev)
                headnorm_fm(lambda c: qf[:, c, 0:N], "qf", 8, N, bones_bf[:, :], "bones_bf", HD, "q_norm_g", lambda c: qn[:, c, 0:N], "qn")
                headnorm_fm(lambda c: kf[:, c, 0:N], "kf", 4, N, bones_bf[:, :], "bones_bf", HD, "k_norm_g", lambda c: kn_bf[:, c, 0:N], "kn_bf",
                            lambda c: knf[:, c, 0:N], "knf")

            def stage_outproj(tiles, xtile_fn, xkey_fn):
                def ev(tid, ntok, c0, cn, b):
                    xa = xtile_fn(tid)
                    A("dve", lambda e: e.tensor_tensor(out=xa[0:ntok, c0:c0 + cn], in0=ps[b][0:ntok, 0:cn], in1=xa[0:ntok, c0:c0 + cn], op=ALU.add),
                      r=["ps%d" % b, xkey_fn(tid)], w=[xkey_fn(tid)])
                return ev

            def stage_rest(N, tiles, xtile_fn, xkey_fn, tok_of, cross_fn, ffn_conv_fn):
                ev_res = stage_outproj(tiles, xtile_fn, xkey_fn)
                linear_tm(w_out, 24, D, 512, lambda kc, tid: mixT[:, kc, tok_of(tid):tok_of(tid) + dict(tiles)[tid]], ["mixT"], tiles, ev_res)
                for (tid, ntok) in tiles:
                    norm_T(xtile_fn(tid)[0:ntok, :], xkey_fn(tid), ntok, "g_cross", lambda kc, tid=tid, ntok=ntok: hT[:, kc, tok_of(tid):tok_of(tid) + ntok], "hT")

                def ev_q(c, m, b):
                    A("act", lambda e: e.copy(out=qcf[:, c // 128, 0:N], in_=ps[b][:, 0:N]), r=["ps%d" % b], w=["qcf"])
                linear_fm(w_cq, KC, 0, D_X, lambda kc: hT[:, kc, 0:N], ["hT"], N, ev_q)
                headnorm_fm(lambda c: qcf[:, c, 0:N], "qcf", 4, N, ones_bf[:, :], "ones_bf", MEM_D, "cq_norm_g", lambda c: qcn[:, c, 0:N], "qcn")
                cross_fn()
                linear_tm(w_co, 4, D, 512, lambda kc, tid: ocT[:, kc, tok_of(tid):tok_of(tid) + dict(tiles)[tid]], ["ocT"], tiles, ev_res)
                for (tid, ntok) in tiles:
                    norm_T(xtile_fn(tid)[0:ntok, :], xkey_fn(tid), ntok, "g_ffn", lambda kc, tid=tid, ntok=ntok: hT[:, kc, tok_of(tid):tok_of(tid) + ntok], "hT")

                def ev_up(c, m, b):
                    ffn_conv_fn(c // 128, b)
                linear_fm(w_up, KC, 0, 2 * DFF, lambda kc: hT[:, kc, 0:N], ["hT"], N, ev_up)
                linear_tm(w_down, FC, D, 512, lambda kc, tid: gT[:, kc, tok_of(tid):tok_of(tid) + dict(tiles)[tid]], ["gT"], tiles, ev_res)


            def ffn_post(ch, N):
                if ch < FC:
                    A("act", lambda e: e.activation(out=gT[:, ch, 0:N], in_=cacc[:, 0:N], func=AF.Silu), r=["cacc"], w=["gT"])
                else:
                    A("dve", lambda e: e.tensor_tensor(out=gT[:, ch - FC, 0:N], in0=gT[:, ch - FC, 0:N], in1=cacc[:, 0:N], op=ALU.mult),
                      r=["cacc", "gT"], w=["gT"])

            A("dve", lambda e: e.memset(convc[:], 0.0), w=["convc"])
            A("dve", lambda e: e.memset(uc[:], 0.0), w=["uc"])
            A("dve", lambda e: e.memset(hst_f[:], 0.0), w=["hst_f"])
            A("dve", lambda e: e.memset(hst_bf[:], 0.0), w=["hst_bf"])

            for gi in range(NT // NTG):
                N = NG
                tiles = [(t, 128) for t in range(NTG)]
                for t in range(NTG):
                    gt = gi * NTG + t
                    A("sp", lambda e, t=t, gt=gt: e.dma_start(out=xt[t][:], in_=xp[gt * 128:(gt + 1) * 128, :]), w=["xt%d" % t], dma=True)
                    norm_T(xt[t][:], "xt%d" % t, 128, "g_mix", lambda kc, t=t: hT[:, kc, t * 128:(t + 1) * 128], "hT")

                def conv_ssd(ch, b, N=N):
                    conv_fm(b, 128, N, 4, "ssd_conv_w", "ssd_conv_b", ch, convc[:, ch, :], "convc", xc[:, ch, 0:N], "xc", AF.Silu)
                stage_inproj(N, conv_ssd)
                ck(3)
                for t in range(NTG):
                    gt = gi * NTG + t
                    sl = slice(t * 128, (t + 1) * 128)
                    b = bank()
                    for c2 in range(2):
                        A("pe", lambda e, c2=c2, b=b, sl=sl: e.transpose(out=ps[b][:, c2 * 128:(c2 + 1) * 128], in_=vTf[:, c2, sl], identity=ident_f),
                          r=["vTf", "cst"], w=["ps%d" % b])
                    for dup in range(2):
                        A("act", lambda e, dup=dup, b=b: e.copy(out=vd_cur[:, :, dup, :], in_=ps[b][:, 0:256].rearrange("p (k d) -> p k d", k=4)),
                          r=["ps%d" % b], w=["vd_cur"])
                    if gt == NT - 1:
                        A("act", lambda e, b=b: e.copy(out=otok[:, 0:256], in_=ps[b][:, 0:256]), r=["ps%d" % b], w=["otok"])
                        A("sp", lambda e: e.dma_start(out=o_vp[:, :], in_=otok[:, 0:256]), r=["otok"], dma=True)
                        b2 = bank()
                        for kvh in range(4):
                            A("pe", lambda e, kvh=kvh, b2=b2, sl=sl: e.transpose(out=ps[b2][:, kvh * 128:(kvh + 1) * 128], in_=knf[:, kvh, sl], identity=ident_f),
                              r=["knf", "cst"], w=["ps%d" % b2])
                        A("act", lambda e, b2=b2: e.copy(out=otok[:, 256:512].rearrange("p (k d) -> p k d", k=4),
                                                         in_=ps[b2][:, :].rearrange("p (k d) -> p k d", k=4)[:, :, 0:64]), r=["ps%d" % b2], w=["otok"])
                        A("sp", lambda e: e.dma_start(out=o_kp[:, :], in_=otok[:, 256:512]), r=["otok"], dma=True)
                    ck(40)
                    attn_tile(lambda c, sl=sl: qn[:, c, sl], "qn", 128, lambda kvh, sl=sl: kn_bf[:, kvh, sl], "kn_bf", vd_cur, "vd_cur", 128,
                              lambda kvh: kd_prev[:, kvh, :], "kd_prev", vd_prev, "vd_prev", (128 if gt > 0 else 0),
                              lambda c, half, sl=sl: mixT[half * 64:(half + 1) * 64, c, sl], "mixT", True)
                    A("act", lambda e, sl=sl: e.copy(out=kd_prev[:], in_=kn_bf[:, :, sl]), r=["kn_bf"], w=["kd_prev"])
                    A("act", lambda e: e.copy(out=vd_prev[:], in_=vd_cur[:]), r=["vd_cur"], w=["vd_prev"])
                    ck(4)
                    ssd_tile(128, lambda j, sl=sl: xc[:, j, sl], "xc", dtT[:, sl], dtaT[:, sl], "dtT",
                             lambda sl=sl: zs[:, :, sl], "zs", lambda j, sl=sl: mixT[:, 8 + j, sl], "mixT")
                    ck(5)

                def cross_p(N=N):
                    cross_attn(N, memkT, "memkT", memv, "memv")

                def ffn_conv_p(ch, b, N=N):
                    conv_fm(b, 128, N, 3, "ffn_conv_w", "ffn_conv_b", ch, uc[:, ch, :], "uc", None, None, None)
                    ffn_post(ch, N)
                stage_rest(N, tiles, lambda tid: xt[tid], lambda tid: "xt%d" % tid, lambda tid: tid * 128, cross_p, ffn_conv_p)
                ck(6)
                for t in range(NTG):
                    gt = gi * NTG + t
                    A("sp", lambda e, t=t, gt=gt: e.dma_start(out=yp[gt * 128:(gt + 1) * 128, :], in_=xt[t][:]), r=["xt%d" % t], dma=True)

            for j in range(0, 16, 4):
                b = bank()
                for i in range(4):
                    A("pe", lambda e, j=j, i=i, b=b: e.transpose(out=ps[b][:, i * 128:(i + 1) * 128], in_=hst_f[:, (j + i) * 128:(j + i + 1) * 128], identity=ident_f),
                      r=["hst_f", "cst"], w=["ps%d" % b])
                A("act", lambda e, b=b: e.copy(out=otok[:, :], in_=ps[b][:, :]), r=["ps%d" % b], w=["otok"])
                A("sp", lambda e, j=j: e.dma_start(out=o_sp[j * 128:(j + 4) * 128, :].rearrange("(i p) n -> p i n", p=128),
                                                   in_=otok[:, :].rearrange("p (i n) -> p i n", i=4)), r=["otok"], dma=True)
            rows_out(lambda ch: convc[:, ch, :], "convc", 24, 3, lambda c0, cn: o_cp[:, c0 * 128:(c0 + cn) * 128])
            rows_out(lambda ch: uc[:, ch, :], "uc", FC2, 2, lambda c0, cn: o_fp[:, c0 * 128:(c0 + cn) * 128])

            ck(7)
            N = NS
            A("dve", lambda e: e.memset(xt[0][:], 0.0), w=["xt0"])
            A("sp", lambda e: e.dma_start(out=xt[0][0:NS, :], in_=xs_in[:, :]), w=["xt0"], dma=True)
            norm_T(xt[0][0:NS, :], "xt0", NS, "g_mix", lambda kc: hT[:, kc, 0:NS], "hT")
            pastT = sb("pastT", [128, 24, NS, 3])
            for c0 in range(0, 24, 8):
                A("sp", lambda e, c0=c0: e.dma_start(out=stg[0:NS * 3, 0:1024], in_=sconv_in.rearrange("b k c -> (b k) c")[:, c0 * 128:(c0 + 8) * 128]),
                  w=["stg"], dma=True)
                for i0_ in range(0, 8, 4):
                    b = bank()
                    for i in range(4):
                        A("pe", lambda e, i0_=i0_, i=i, b=b: e.transpose(out=ps[b][:, i * 128:i * 128 + NS * 3], in_=stg[0:NS * 3, (i0_ + i) * 128:(i0_ + i + 1) * 128],
                                                                        identity=ident_f[0:NS * 3, 0:NS * 3]), r=["stg", "cst"], w=["ps%d" % b])
                    A("act", lambda e, c0=c0, i0_=i0_, b=b: e.copy(out=pastT[:, c0 + i0_:c0 + i0_ + 4, :, :].rearrange("p c b k -> p c (b k)"),
                                                               in_=ps[b][:, :].rearrange("p (c x) -> p c x", c=4)[:, :, 0:NS * 3]), r=["ps%d" % b], w=["pastT"])
            xraw = sb("xraw", [128, 24, NS])

            def conv_s(ch, b):
                wo = cfg.po["ssd_conv_w"][0] + ch * 4
                A("act", lambda e: e.copy(out=xraw[:, ch, :], in_=ps[b][:, 0:NS]), r=["ps%d" % b], w=["xraw"])
                A("dve", lambda e: e.tensor_scalar(out=cacc[:, 0:NS], in0=ps[b][:, 0:NS], scalar1=prm[:, wo + 3:wo + 4], scalar2=pcol("ssd_conv_b", ch),
                                                   op0=ALU.mult, op1=ALU.add), r=["ps%d" % b, "prm"], w=["cacc"])
                for k in range(3):
                    A("dve", lambda e, k=k: e.scalar_tensor_tensor(out=cacc[:, 0:NS], in0=pastT[:, ch, :, k], scalar=prm[:, wo + k:wo + k + 1], in1=cacc[:, 0:NS],
                                                                   op0=ALU.mult, op1=ALU.add), r=["pastT", "cacc", "prm"], w=["cacc"])
                A("act", lambda e: e.activation(out=xc[:, ch, 0:NS], in_=cacc[:, 0:NS], func=AF.Silu), r=["cacc"], w=["xc"])
            stage_inproj(NS, conv_s)
            A("sp", lambda e: e.dma_start(out=o_cs[:, 0:2, :], in_=sconv_in[:, 1:3, :]), dma=True)
            rows_out(lambda ch: xraw[:, ch, :], "xraw", 24, NS, lambda c0, cn: o_cs[:, 2, c0 * 128:(c0 + cn) * 128])
            A("sp", lambda e: e.dma_start(out=o_ks[:, 0:127, :], in_=ck_in[:, 1:128, :]), dma=True)
            A("sp", lambda e: e.dma_start(out=o_vs[:, 0:127, :], in_=cv_in[:, 1:128, :]), dma=True)
            A("sp", lambda e: e.dma_start(out=o_fs[:, 0, :], in_=sffn_in[:, 1, :]), dma=True)

            ck(8)
            kst = sb("kst", [128, D_KV]); vst = sb("vst", [128, D_KV]); kdup = sb("kdup", [128, 4, 2, 64], BF16)
            sst = stg[:, :].rearrange("p (j n) -> p j n", j=16); cmk_bf = sb("cmk_bf", [128, MB, D_X], BF16)
            smkT = sb("smkT", [128, 4, NMEM], BF16); smv = sb("smv", [128, MB, D_X], BF16)
            vrow = sb("vrow", [1, 4, 2, 64], BF16); krow_f = sb("krow_f", [1, 256]); vrow_f = sb("vrow_f", [1, 256])
            mixS = sb("mixS", [128, 24, NS], BF16)
            for s in range(NS):
                A("sp", lambda e, s=s: e.dma_start(out=kst[:], in_=ck_in[s, :, :]), w=["kst"], dma=True)
                A("sp", lambda e, s=s: e.dma_start(out=vst[:], in_=cv_in[s, :, :]), w=["vst"], dma=True)
                for dup in range(2):
                    A("dve", lambda e, dup=dup: e.tensor_copy(out=kdup[:, :, dup, :], in_=kst[:, :].rearrange("p (k d) -> p k d", k=4)), r=["kst"], w=["kdup"])
                    A("dve", lambda e, dup=dup: e.tensor_copy(out=vd_prev[:, :, dup, :], in_=vst[:, :].rearrange("p (k d) -> p k d", k=4)), r=["vst"], w=["vd_prev"])
                b = bank()
                for kvh in range(4):
                    A("pe", lambda e, kvh=kvh, b=b: e.transpose(out=psbf(b)[:, kvh * 128:(kvh + 1) * 128], in_=kdup[:, kvh, :, :].rearrange("p a d -> p (a d)"),
                                                                identity=ident_bf[:, :]), r=["kdup", "ident_bf"], w=["ps%d" % b])
                A("act", lambda e, b=b: e.copy(out=kd_prev[:, :, :].rearrange("p k s -> p (k s)"), in_=psbf(b)[:, 0:512]), r=["ps%d" % b], w=["kd_prev"])
                b = bank()
                for c2 in range(2):
                    A("pe", lambda e, c2=c2, b=b, s=s: e.transpose(out=ps[b][0:1, c2 * 128:(c2 + 1) * 128], in_=vTf[:, c2, s:s + 1], identity=ident_f),
                      r=["vTf", "cst"], w=["ps%d" % b])
                for kvh in range(4):
                    A("pe", lambda e, kvh=kvh, b=b, s=s: e.transpose(out=ps[b][0:1, 256 + kvh * 64:256 + (kvh + 1) * 64], in_=knf[0:64, kvh, s:s + 1], identity=ident_f[0:64, 0:64]),
                      r=["knf", "cst"], w=["ps%d" % b])
                for dup in range(2):
                    A("act", lambda e, dup=dup, b=b: e.copy(out=vrow[0:1, :, dup, :], in_=ps[b][0:1, 0:256].rearrange("p (k d) -> p k d", k=4)), r=["ps%d" % b], w=["vrow"])
                A("act", lambda e, b=b: e.copy(out=vrow_f[0:1, :], in_=ps[b][0:1, 0:256]), r=["ps%d" % b], w=["vrow_f"])
                A("act", lambda e, b=b: e.copy(out=krow_f[0:1, 0:256], in_=ps[b][0:1, 256:512]), r=["ps%d" % b], w=["krow_f"])
                A("sp", lambda e, s=s: e.dma_start(out=o_vs[s, 127:128, :], in_=vrow_f[0:1, :]), r=["vrow_f"], dma=True)
                A("sp", lambda e, s=s: e.dma_start(out=o_ks[s, 127:128, :], in_=krow_f[0:1, 0:256]), r=["krow_f"], dma=True)
                attn_tile(lambda c, s=s: qn[:, c, s:s + 1], "qn", 1, lambda kvh, s=s: kn_bf[:, kvh, s:s + 1], "kn_bf", vrow, "vrow", 1,
                          lambda kvh: kd_prev[:, kvh, :], "kd_prev", vd_prev, "vd_prev", 128,
                          lambda c, half, s=s: mixS[half * 64:(half + 1) * 64, c, s:s + 1], "mixS", False)
                A("sp", lambda e, s=s: e.dma_start(out=sst, in_=sst_in[s, :, :].rearrange("(j p) n -> p j n", p=128)), w=["stg"], dma=True)
                for j0 in range(0, 16, 4):
                    b = bank()
                    for i in range(4):
                        A("pe", lambda e, j0=j0, i=i, b=b: e.transpose(out=ps[b][:, i * 128:(i + 1) * 128], in_=sst[:, j0 + i, :], identity=ident_f), r=["stg", "cst"], w=["ps%d" % b])
                    A("act", lambda e, j0=j0, b=b: e.copy(out=hst_f[:, j0 * 128:(j0 + 4) * 128], in_=ps[b][:, :]), r=["ps%d" % b], w=["hst_f"])
                A("dve", lambda e: e.tensor_copy(out=hst_bf[:], in_=hst_f[:]), r=["hst_f"], w=["hst_bf"])
                ssd_tile(1, lambda j, s=s: xc[:, j, s:s + 1], "xc", dtT[:, s:s + 1], dtaT[:, s:s + 1], "dtT",
                         lambda s=s: zs[:, :, s:s + 1], "zs", lambda j, s=s: mixS[:, 8 + j, s:s + 1], "mixS")
                for j0 in range(0, 16, 4):
                    b = bank()
                    for i in range(4):
                        A("pe", lambda e, j0=j0, i=i, b=b: e.transpose(out=ps[b][:, i * 128:(i + 1) * 128], in_=hst_f[:, (j0 + i) * 128:(j0 + i + 1) * 128], identity=ident_f),
                          r=["hst_f", "cst"], w=["ps%d" % b])
                    A("act", lambda e, j0=j0, b=b: e.copy(out=sst[:, j0:j0 + 4, :].rearrange("p j n -> p (j n)"), in_=ps[b][:, :]), r=["ps%d" % b], w=["stg"])
                A("sp", lambda e, s=s: e.dma_start(out=o_ss[s, :, :].rearrange("(j p) n -> p j n", p=128), in_=sst), r=["stg"], dma=True)
            A("act", lambda e: e.copy(out=mixT[:, :, 0:NS], in_=mixS[:]), r=["mixS"], w=["mixT"])
            ck(9)

            fpast = sb("fpast", [128, 16, NS, 2]); uraw = sb("uraw", [128, 16, NS])
            fst = stg

            def load_fpast(c0):
                cn = min(16, FC2 - c0)
                A("sp", lambda e: e.dma_start(out=fst[0:NS * 2, 0:cn * 128], in_=sffn_in.rearrange("b k c -> (b k) c")[:, c0 * 128:(c0 + cn) * 128]),
                  w=["stg"], dma=True)
                for i0 in range(0, cn, 4):
                    b = bank()
                    n4 = min(4, cn - i0)
                    for i in range(n4):
                        A("pe", lambda e, i0=i0, i=i, b=b: e.transpose(out=ps[b][:, i * 128:i * 128 + NS * 2], in_=fst[0:NS * 2, (i0 + i) * 128:(i0 + i + 1) * 128],
                                                                      identity=ident_f[0:NS * 2, 0:NS * 2]), r=["stg", "cst"], w=["ps%d" % b])
                    A("act", lambda e, i0=i0, n4=n4, b=b: e.copy(out=fpast[:, i0:i0 + n4, :, :].rearrange("p c b k -> p c (b k)"),
                                                                 in_=ps[b][:, :].rearrange("p (c x) -> p c x", c=4)[:, 0:n4, 0:NS * 2]), r=["ps%d" % b], w=["fpast"])

            def cross_s():
                for s in range(NS):
                    A("pool", lambda e, s=s: e.dma_start(out=cmk_bf[:], in_=cmk_in[s, :, :].rearrange("(m p) c -> p m c", p=128)), w=["cmk_bf"], dma=True)
                    A("pool", lambda e, s=s: e.dma_start(out=smv[:], in_=cmv_in[s, :, :].rearrange("(m p) c -> p m c", p=128)), w=["smv"], dma=True)
                    for mb in range(MB):
                        b = bank()
                        for h in range(4):
                            A("pe", lambda e, h=h, mb=mb, b=b: e.transpose(out=psbf(b)[:, h * 128:(h + 1) * 128], in_=cmk_bf[:, mb, h * 128:(h + 1) * 128], identity=ident_bf[:, :]),
                              r=["cmk_bf", "ident_bf"], w=["ps%d" % b])
                        A("act", lambda e, mb=mb, b=b: e.copy(out=smkT[:, :, mb * 128:(mb + 1) * 128], in_=psbf(b)[:, 0:512].rearrange("p (h m) -> p h m", h=4)),
                          r=["ps%d" % b], w=["smkT"])
                    cross_attn_one(s)

            def cross_attn_one(s):
                for h in range(4):
                    for mb in range(MB):
                        b = bank()
                        A("pe", lambda e, h=h, mb=mb, b=b: e.matmul(ps[b][:, 0:1], lhsT=smkT[:, h, mb * 128:(mb + 1) * 128], rhs=qcn[:, h, s:s + 1], start=True, stop=True),
                          r=["smkT", "qcn"], w=["ps%d" % b])
                        A("act", lambda e, mb=mb, b=b: e.activation(out=pc[mb % 2][:, 0:1], in_=ps[b][:, 0:1], func=AF.Exp, scale=MEM_D ** -0.5),
                          r=["ps%d" % b], w=["pc%d" % (mb % 2)])
                    bo, bd = bank(), bank()
                    for mb in range(MB):
                        A("pe", lambda e, h=h, mb=mb, bo=bo: e.matmul(ps[bo][:, 0:1], lhsT=smv[:, mb, h * 128:(h + 1) * 128], rhs=pc[mb % 2][:, 0:1],
                                                                     start=(mb == 0), stop=(mb == MB - 1)), r=["smv", "pc%d" % (mb % 2)], w=["ps%d" % bo])
                        A("pe", lambda e, mb=mb, bd=bd: e.matmul(ps[bd][:, 0:1], lhsT=ones_bf[:, :], rhs=pc[mb % 2][:, 0:1],
                                                                start=(mb == 0), stop=(mb == MB - 1)), r=["ones_bf", "pc%d" % (mb % 2)], w=["ps%d" % bd])
                    A("dve", lambda e, bd=bd: e.reciprocal(out=nrm_r[:, 0:1], in_=ps[bd][:, 0:1]), r=["ps%d" % bd], w=["nrm_r"])
                    A("dve", lambda e, h=h, bo=bo: e.tensor_tensor(out=ocT[:, h, s:s + 1], in0=ps[bo][:, 0:1], in1=nrm_r[:, 0:1], op=ALU.mult),
                      r=["ps%d" % bo, "nrm_r"], w=["ocT"])

            def ffn_conv_s(ch, b):
                wo = cfg.po["ffn_conv_w"][0] + ch * 3
                cl = ch % 16
                if cl == 0:
                    load_fpast(ch)
                A("act", lambda e: e.copy(out=uraw[:, cl, :], in_=ps[b][:, 0:NS]), r=["ps%d" % b], w=["uraw"])
                A("dve", lambda e: e.tensor_scalar(out=cacc[:, 0:NS], in0=ps[b][:, 0:NS], scalar1=prm[:, wo + 2:wo + 3], scalar2=pcol("ffn_conv_b", ch),
                                                   op0=ALU.mult, op1=ALU.add), r=["ps%d" % b, "prm"], w=["cacc"])
                for k in range(2):
                    A("dve", lambda e, k=k: e.scalar_tensor_tensor(out=cacc[:, 0:NS], in0=fpast[:, cl, :, k], scalar=prm[:, wo + k:wo + k + 1], in1=cacc[:, 0:NS],
                                                                   op0=ALU.mult, op1=ALU.add), r=["fpast", "cacc", "prm"], w=["cacc"])
                ffn_post(ch, NS)
                if cl == 15 or ch == FC2 - 1:
                    base = ch - cl
                    rows_out(lambda chh: uraw[:, chh, :], "uraw", cl + 1, NS, lambda c0, cn, base=base: o_fs[:, 1, (base + c0) * 128:(base + c0 + cn) * 128])
            stage_rest(NS, [(0, NS)], lambda tid: xt[0], lambda tid: "xt0", lambda tid: 0, cross_s, ffn_conv_s)
            A("sp", lambda e: e.dma_start(out=ys[:, :], in_=xt[0][0:NS, :]), r=["xt0"], dma=True)


        except _Stop:
            pass

        with nc.Block() as block:
            fns = {"pe": block.tensor, "act": block.scalar, "dve": block.vector, "pool": block.gpsimd, "sp": block.sync}
            P.emit(fns)
    return nc


def make_consts():
    p = np.arange(128)[:, None]; c = np.arange(128)[None, :]
    ident = (p == c); mcur = (p <= c); mprev = (p >= c); U = (p > c); bones = (p // 64 == c // 64)
    return np.concatenate([ident, mcur, mprev, U, bones], axis=1).astype(np.float32)


def make_params(cfg, inp):
    def fm(v):
        return np.ascontiguousarray(np.asarray(v, np.float32).reshape(-1, 128).T)
    cols = {}
    for nme in ("g_mix", "g_cross", "g_ffn", "g_mem", "ssd_norm_g"):
        cols[nme] = fm(inp[nme][0])
    cw = np.asarray(inp["ssd_conv_w"][0], np.float32)
    cols["ssd_conv_w"] = np.ascontiguousarray(cw.reshape(4, 24, 128).transpose(2, 1, 0).reshape(128, 96))
    cols["ssd_conv_b"] = fm(inp["ssd_conv_b"][0])
    fw = np.asarray(inp["ffn_conv_w"][0], np.float32)
    cols["ffn_conv_w"] = np.ascontiguousarray(fw.reshape(3, cfg.FC2, 128).transpose(2, 1, 0).reshape(128, 3 * cfg.FC2))
    cols["ffn_conv_b"] = fm(inp["ffn_conv_b"][0])
    cols["q_norm_g"] = np.tile(np.asarray(inp["q_norm_g"][0], np.float32), 2)[:, None]
    cols["k_norm_g"] = np.tile(np.asarray(inp["k_norm_g"][0], np.float32), 2)[:, None]
    cols["cq_norm_g"] = np.asarray(inp["cq_norm_g"][0], np.float32)[:, None]
    cols["ck_norm_g"] = np.asarray(inp["ck_norm_g"][0], np.float32)[:, None]
    cols["sinks"] = np.tile(np.asarray(inp["sinks"][0], np.float32)[None, :], (128, 1))
    z = np.zeros((128, 1), np.float32); z[:32, 0] = inp["dt_bias"][0]; cols["dt_bias"] = z
    z = np.zeros((128, 1), np.float32); z[:32, 0] = inp["a_log"][0]; cols["a_log"] = z
    cols["d_skip"] = fm(np.repeat(np.asarray(inp["d_skip"][0], np.float32), 64))
    out = np.zeros((128, cfg.PW), np.float32)
    for nme, (o, w) in cfg.po.items():
        assert cols[nme].shape == (128, w), (nme, cols[nme].shape, w)
        out[:, o:o + w] = cols[nme]
    return out


def make_in_maps(cfg, inp, n_cores, n_batch):
    consts = make_consts(); params = make_params(cfg, inp)
    NS = cfg.NS
    maps = []
    f = lambda a: np.ascontiguousarray(np.asarray(a, np.float32))
    for c in range(n_cores):
        b = c % n_batch
        s0 = c * NS
        m = {
            "x_prompt": f(inp["x_prompt"][b]), "x_sample": f(inp["x_sample"][s0:s0 + NS, 0]),
            "cache_swa_k": f(inp["cache_swa_k"][0, s0:s0 + NS]).reshape(NS, 128, D_KV),
            "cache_swa_v": f(inp["cache_swa_v"][0, s0:s0 + NS]).reshape(NS, 128, D_KV),
            "state_ssd_conv": f(inp["state_ssd_conv"][0, s0:s0 + NS]),
            "state_ssd": f(inp["state_ssd"][0, s0:s0 + NS]).reshape(NS, D_SSD, DST),
            "cache_mem_k": f(inp["cache_mem_k"][0, s0:s0 + NS]).reshape(NS, cfg.NMEM, D_X),
            "cache_mem_v": f(inp["cache_mem_v"][0, s0:s0 + NS]).reshape(NS, cfg.NMEM, D_X),
            "state_ffn_conv": f(inp["state_ffn_conv"][0, s0:s0 + NS]), "mem_prompt": f(inp["mem_prompt"][b]),
            "w_in": f(inp["w_in"][0]), "w_out": f(inp["w_out"][0]), "w_cq": f(inp["w_cq"][0]), "w_ck": f(inp["w_ck"][0]),
            "w_cv": f(inp["w_cv"][0]), "w_co": f(inp["w_co"][0]), "w_up": f(inp["w_up"][0]), "w_down": f(inp["w_down"][0]),
            "consts": consts, "params": params,
        }
        maps.append(m)
    return maps


def assemble(cfg, res, n_cores, n_batch):
    NS = cfg.NS
    B = n_batch
    g = lambda name, c: np.asarray(res[c][name], np.float32)
    cat_s = lambda name, shp: np.concatenate([g(name, c).reshape((NS,) + shp) for c in range(n_cores)], axis=0)
    st_p = lambda name, shp: np.stack([g(name, c).reshape(shp) for c in range(B)], axis=0)
    return (
        st_p("y_prompt", (cfg.SEQ, cfg.D)), cat_s("y_sample", (1, cfg.D)),
        st_p("swa_k_prompt", (128, N_KV, HD))[None], st_p("swa_v_prompt", (128, N_KV, HD))[None],
        cat_s("swa_k_sample", (128, N_KV, HD))[None], cat_s("swa_v_sample", (128, N_KV, HD))[None],
        st_p("ssd_conv_prompt", (3, CONV_DIM))[None], cat_s("ssd_conv_sample", (3, CONV_DIM))[None],
        st_p("ssd_state_prompt", (SSD_H, SSD_P, DST))[None], cat_s("ssd_state_sample", (SSD_H, SSD_P, DST))[None],
        st_p("mem_k_prompt", (cfg.NMEM, MEM_H, MEM_D))[None], st_p("mem_v_prompt", (cfg.NMEM, MEM_H, MEM_D))[None],
        st_p("ffn_conv_prompt", (2, 2 * cfg.DFF))[None], cat_s("ffn_conv_sample", (2, 2 * cfg.DFF))[None],
    )


def kernel(**inputs):
    cfg = Cfg()
    nc = build(cfg)
    maps = make_in_maps(cfg, inputs, 8, 4)
    res = run_bass_kernel_spmd(nc, maps, core_ids=list(range(8)))
    return assemble(cfg, res.results, 8, 4)
```
